# Optimizing a Trainium2 kernel written in Bass

```python
import math
import jax, jax.numpy as jnp
from jax import lax
import numpy as np

D_MODEL = 2048
BATCH = 8
SEQ = 2048
DEPTH = 1

D_MIX = D_MODEL
SB_HEADS = 8
SB_HEAD_DIM = 128
SB_WIDTH = SB_HEADS * SB_HEAD_DIM
DF_HEADS = 8
DF_QK_DIM = 64
DF_V_DIM = 2 * DF_QK_DIM
DF_WIDTH = DF_HEADS * DF_V_DIM
ROPE_DIM = DF_QK_DIM // 4
ROPE_THETA = 500000.0
PLE_DIM = 256
Q_BLOCK = 128
NORM_EPS = 1e-6
SUBLN_EPS = 1e-5
LAMBDA_STD = 0.1
IN_SIZES = (SB_WIDTH, SB_WIDTH, SB_WIDTH, SB_WIDTH,
            DF_HEADS * 2 * DF_QK_DIM, DF_HEADS * 2 * DF_QK_DIM,
            DF_WIDTH, DF_WIDTH)
IN_COLS = sum(IN_SIZES)

kernel_name = "hybrid_stickbreak_diffattn_ple"


def rmsnorm(x, g, eps=NORM_EPS):
    xf = x.astype(jnp.float32)
    y = xf * lax.rsqrt(jnp.mean(xf * xf, axis=-1, keepdims=True) + eps)
    return (y * g.astype(jnp.float32)).astype(x.dtype)


def partial_rope(x, pos):
    half = ROPE_DIM // 2
    inv_freq = ROPE_THETA ** (-jnp.arange(0, ROPE_DIM, 2, dtype=jnp.float32) / ROPE_DIM)
    ang = pos.astype(jnp.float32)[:, None] * inv_freq[None, :]
    cos = jnp.cos(ang)[:, None, None, :].astype(x.dtype)
    sin = jnp.sin(ang)[:, None, None, :].astype(x.dtype)
    x1 = x[..., :half]
    x2 = x[..., half:ROPE_DIM]
    rot = jnp.concatenate([x1 * cos - x2 * sin, x2 * cos + x1 * sin], axis=-1)
    return jnp.concatenate([rot, x[..., ROPE_DIM:]], axis=-1)


def stick_breaking_attention(q, k, v):
    B, S, H, Dh = q.shape
    nb = S // Q_BLOCK
    scale = Dh ** -0.5
    kpos = jnp.arange(S)
    qb = q.reshape(B, nb, Q_BLOCK, H, Dh).transpose(1, 0, 2, 3, 4)

    def block(args):
        qi, bi = args
        z = jnp.einsum('bqhd,bkhd->bhqk', qi, k).astype(jnp.float32) * scale
        qpos = bi * Q_BLOCK + jnp.arange(Q_BLOCK)
        mask = kpos[None, :] < qpos[:, None]
        log_not_beta = jnp.where(mask, -jax.nn.softplus(z), 0.0)
        later = lax.cumsum(log_not_beta, axis=3, reverse=True) - log_not_beta
        w = jnp.where(mask, jnp.exp(jax.nn.log_sigmoid(z) + later), 0.0)
        return jnp.einsum('bhqk,bkhd->bqhd', w.astype(v.dtype), v)

    out = lax.map(block, (qb, jnp.arange(nb)))
    return out.transpose(1, 0, 2, 3, 4).reshape(B, S, H, Dh)


def differential_attention(q, k, v, lam):
    B, S, H, _, dq = q.shape
    nb = S // Q_BLOCK
    scale = dq ** -0.5
    kpos = jnp.arange(S)
    qb = q.reshape(B, nb, Q_BLOCK, H, 2, dq).transpose(1, 0, 2, 3, 4, 5)

    def block(args):
        qi, bi = args
        s = jnp.einsum('bqhcd,bkhcd->bhcqk', qi, k).astype(jnp.float32) * scale
        qpos = bi * Q_BLOCK + jnp.arange(Q_BLOCK)
        mask = kpos[None, :] <= qpos[:, None]
        pr = jax.nn.softmax(jnp.where(mask, s, -jnp.inf), axis=-1)
        a = pr[:, :, 0] - lam * pr[:, :, 1]
        return jnp.einsum('bhqk,bkhd->bqhd', a.astype(v.dtype), v)

    out = lax.map(block, (qb, jnp.arange(nb)))
    return out.transpose(1, 0, 2, 3, 4).reshape(B, S, H, v.shape[-1])


def setup_inputs(seed: int = 0) -> dict:
    key = jax.random.key(seed)
    ks = jax.random.split(key, 16)
    f32 = jnp.float32
    nrm = lambda k, shape, scale: jax.random.normal(k, shape, f32) * scale
    return {
        "x": nrm(ks[0], (BATCH, SEQ, D_MODEL), 1.0),
        "p": nrm(ks[1], (DEPTH, BATCH, SEQ, PLE_DIM), 1.0),
        "norm_mix_g": 1.0 + nrm(ks[2], (DEPTH, D_MODEL), 0.02),
        "w_in": nrm(ks[3], (DEPTH, D_MODEL, IN_COLS), D_MODEL ** -0.5),
        "lambda_q1": nrm(ks[4], (DEPTH, DF_QK_DIM), LAMBDA_STD),
        "lambda_k1": nrm(ks[5], (DEPTH, DF_QK_DIM), LAMBDA_STD),
        "lambda_q2": nrm(ks[6], (DEPTH, DF_QK_DIM), LAMBDA_STD),
        "lambda_k2": nrm(ks[7], (DEPTH, DF_QK_DIM), LAMBDA_STD),
        "subln_g": 1.0 + nrm(ks[8], (DEPTH, DF_V_DIM), 0.02),
        "w_out": nrm(ks[9], (DEPTH, D_MIX, D_MODEL), D_MIX ** -0.5),
        "norm_ple_g": 1.0 + nrm(ks[10], (DEPTH, D_MODEL), 0.02),
        "w_ple_gate": nrm(ks[11], (DEPTH, D_MODEL, D_MODEL), D_MODEL ** -0.5),
        "w_ple_proj": nrm(ks[12], (DEPTH, PLE_DIM, D_MODEL), PLE_DIM ** -0.5),
        "norm_final_g": 1.0 + nrm(ks[13], (D_MODEL,), 0.02),
    }


def reference(x, p, norm_mix_g, w_in, lambda_q1, lambda_k1, lambda_q2, lambda_k2, subln_g,
              w_out, norm_ple_g, w_ple_gate, w_ple_proj, norm_final_g):
    B, S, _ = x.shape
    pos = jnp.arange(S)
    split_at = [int(v) for v in np.cumsum(IN_SIZES)[:-1]]
    h = x
    for i in range(DEPTH):
        lambda_init = 0.8 - 0.6 * math.exp(-0.3 * i)
        hn = rmsnorm(h, norm_mix_g[i])
        proj = hn @ w_in[i]
        sb_q, sb_k, sb_v, sb_g, df_q, df_k, df_v, df_g = jnp.split(proj, split_at, axis=-1)

        sb_shape = (B, S, SB_HEADS, SB_HEAD_DIM)
        sb_out = stick_breaking_attention(sb_q.reshape(sb_shape), sb_k.reshape(sb_shape),
                                          sb_v.reshape(sb_shape)).reshape(B, S, SB_WIDTH)

        qk_shape = (B, S, DF_HEADS, 2, DF_QK_DIM)
        dq = partial_rope(df_q.reshape(qk_shape), pos)
        dk = partial_rope(df_k.reshape(qk_shape), pos)
        lam = (jnp.exp(jnp.sum(lambda_q1[i] * lambda_k1[i]).astype(jnp.float32))
               - jnp.exp(jnp.sum(lambda_q2[i] * lambda_k2[i]).astype(jnp.float32))
               + lambda_init)
        df_out = differential_attention(dq, dk, df_v.reshape(B, S, DF_HEADS, DF_V_DIM), lam)
        df_out = rmsnorm(df_out, subln_g[i], SUBLN_EPS) * (1.0 - lambda_init)
        df_out = df_out.reshape(B, S, DF_WIDTH)

        mixed = jnp.concatenate([sb_out * jax.nn.silu(sb_g), df_out * jax.nn.silu(df_g)], axis=-1)
        h = h + mixed @ w_out[i]

        gate = jax.nn.sigmoid(rmsnorm(h, norm_ple_g[i]) @ w_ple_gate[i])
        h = h + gate * (p[i] @ w_ple_proj[i])
    return rmsnorm(h, norm_final_g)
```

```python
import math
import numpy as np
import concourse.bass as bass
import concourse.mybir as mybir
from concourse.bass_utils import run_bass_kernel_spmd

F32 = mybir.dt.float32
BF16 = mybir.dt.bfloat16
AF = mybir.ActivationFunctionType
ALU = mybir.AluOpType
AX = mybir.AxisListType

D = 2048
KC = 16
PLE = 256
NORM_EPS = 1e-6
SUBLN_EPS = 1e-5
LAMBDA_INIT = 0.8 - 0.6 * math.exp(-0.3 * 0)
SB_SCALE = 128 ** -0.5
DF_SCALE = 64 ** -0.5


class Op:
    __slots__ = ("eng", "fn", "deps", "raw", "is_dma", "slot", "dcount", "needed", "count", "defer", "seq")


class Prog:
    ENGS = ("pe", "act", "dve", "pool", "sp")

    def __init__(self, nc, name):
        self.nc = nc
        self.name = name
        self.ops = {e: [] for e in self.ENGS}
        self.lastw = {}
        self.readers = {}
        self.slot_tot = {}

    def _add(self, eng, fn, r, w, is_dma=False, slot=None, defer=False):
        o = Op()
        o.eng, o.fn, o.is_dma, o.slot = eng, fn, is_dma, slot
        o.defer = defer
        self.nseq = getattr(self, "nseq", 0) + 1
        o.seq = self.nseq
        o.needed = False
        o.count = 0
        o.dcount = 0
        deps = {}
        for k in r:
            lw = self.lastw.get(k)
            if lw is not None:
                deps[id(lw)] = (lw, True)
        for k in w:
            lw = self.lastw.get(k)
            if lw is not None and id(lw) not in deps:
                deps[id(lw)] = (lw, False)
            for rd in self.readers.get(k, ()):
                if id(rd) not in deps:
                    deps[id(rd)] = (rd, False)
        for k in r:
            self.readers.setdefault(k, []).append(o)
        for k in w:
            self.lastw[k] = o
            self.readers[k] = []
        keep = []
        for d, israw in deps.values():
            if d is o:
                continue
            if d.is_dma or is_dma:
                keep.append(d)
            elif d.eng == eng:
                if eng in ("act", "dve") or (eng == "pool" and israw):
                    keep.append(d)
            else:
                keep.append(d)
        o.deps = keep
        for d in keep:
            d.needed = True
        if is_dma:
            self.slot_tot[slot] = self.slot_tot.get(slot, 0) + 16
            o.dcount = self.slot_tot[slot]
        self.ops[eng].append(o)
        return o

    def pe(self, fn, r=(), w=(), defer=False):
        return self._add("pe", fn, r, w, defer=defer)

    def act(self, fn, r=(), w=()):
        return self._add("act", fn, r, w)

    def dve(self, fn, r=(), w=()):
        return self._add("dve", fn, r, w)

    def pool(self, fn, r=(), w=()):
        return self._add("pool", fn, r, w)

    def dma(self, q, fn, slot, r=(), w=()):
        return self._add(q, fn, r, w, is_dma=True, slot=slot)

    def emit(self, sems):
        nc = self.nc
        final = {}
        pel = self.ops["pe"]
        nxt = {}
        last_sig = None
        for o in reversed(pel):
            if not o.defer:
                last_sig = o
            nxt[id(o)] = last_sig
        for e in self.ENGS:
            for o in self.ops[e]:
                o.needed = False
        for e in self.ENGS:
            for o in self.ops[e]:
                nd = []
                for d in o.deps:
                    if (not d.is_dma) and d.eng == "pe" and d.defer:
                        r_ = nxt[id(d)]
                        assert r_ is not None and r_.seq < o.seq, "deferred PE dependency cannot be satisfied"
                        r_.needed = True
                        d = r_
                    if d not in nd:
                        nd.append(d)
                    d.needed = True
                o.deps = nd
        for e in self.ENGS:
            c = 0
            comp = [o for o in self.ops[e] if not o.is_dma]
            if comp:
                comp[-1].needed = True
            for o in self.ops[e]:
                if o.is_dma:
                    continue
                if o.needed:
                    c += 1
                o.count = c
            final[e] = c
        ops = self.ops
        slot_tot = self.slot_tot

        def run(e, eng):
            waited = {}
            for o in ops[e]:
                for d in o.deps:
                    if d.is_dma:
                        key, val = ("slot", d.slot), d.dcount
                    else:
                        key, val = ("eng", d.eng), d.count
                    if waited.get(key, 0) >= val:
                        continue
                    waited[key] = val
                    s = sems[key]
                    eng.wait_ge(s, val)
                ins = o.fn(eng)
                if o.is_dma:
                    ins.then_inc(sems[("slot", o.slot)], 16)
                elif o.needed:
                    ins.then_inc(sems[("eng", e)], 1)
            for e2 in self.ENGS:
                if final[e2] > 0 and waited.get(("eng", e2), 0) < final[e2]:
                    eng.wait_ge(sems[("eng", e2)], final[e2])
            for sl, tot in slot_tot.items():
                if waited.get(("slot", sl), 0) < tot:
                    eng.wait_ge(sems[("slot", sl)], tot)

        with nc.Block() as block:
            @block.tensor
            def _(eng):
                run("pe", eng)

            @block.scalar
            def _(eng):
                run("act", eng)

            @block.vector
            def _(eng):
                run("dve", eng)

            @block.gpsimd
            def _(eng):
                run("pool", eng)

            @block.sync
            def _(eng):
                run("sp", eng)


class SemPool:
    def __init__(self, nc, stack, prefix):
        self.nc, self.stack, self.prefix = nc, stack, prefix
        self.d = {}

    def __getitem__(self, key):
        if key not in self.d:
            nm = self.prefix + "_" + "_".join(str(k) for k in key)
            self.d[key] = self.stack.enter_context(self.nc.semaphore(nm))
        return self.d[key]


def build_nc(S=2048, n_sb=8, n_df=8, debug_mixed=False, nblocks=3, dbg_hn=False, stage=9, sbstage=9, dfstage=9):
    from contextlib import ExitStack
    NT = S // 128
    NTT = S // 512
    NH = 16
    nc = bass.Bass("TRN2", target_bir_lowering=False)

    def din(name, shape):
        return nc.dram_tensor(name, list(shape), F32, kind="ExternalInput").ap()

    x_d = din("x", [S, D])
    p_d = din("p", [S, PLE])
    win_d = din("w_in", [D, 8192])
    wout_d = din("w_out", [D, D])
    wgate_d = din("w_gate", [D, D])
    wproj_d = din("w_proj", [PLE, D])
    gmix_d = din("gmix", [128, KC])
    gple_d = din("gple", [128, KC])
    gfin_d = din("gfin", [128, D])
    subg_d = din("subg", [128, 1])
    lamv_d = din("lamv", [128, 4, 64])
    cos_d = din("cos", [128, NT, 8])
    sin_d = din("sin", [128, NT, 8])
    ident_d = din("ident", [128, 128])
    ntri_d = din("ntri", [128, 128])
    mbase_d = din("mbase", [128, 2048])
    y_d = nc.dram_tensor("y", [S, D], F32, kind="ExternalOutput").ap()
    if debug_mixed or dbg_hn:
        dbg_d = nc.dram_tensor("dbg", [128, NH, S], F32, kind="ExternalOutput").ap()

    wsc_d = nc.dram_tensor("wsc", [2, 8, 128, KC * 256], BF16, kind="Internal").ap()
    win_v = win_d.rearrange("(kc p) c -> p kc c", p=128)
    wout_v = wout_d.rearrange("(kc p) c -> p kc c", p=128)
    wgate_v = wgate_d.rearrange("(kc p) c -> p kc c", p=128)
    wproj_v = wproj_d.rearrange("(kc p) c -> p kc c", p=128)

    with ExitStack() as top:
        def sb(name, shape, dt, stack=top):
            return stack.enter_context(nc.sbuf_tensor("s_" + name, list(shape), dt))

        pb = [top.enter_context(nc.psum_tensor(f"pb{i}", [128, 512], F32)) for i in range(8)]

        def BK(i):
            return ("bank", i)

        idb = sb("idb", [128, 128], BF16)
        ntrib = sb("ntrib", [128, 128], BF16)
        nonesb = sb("nonesb", [128, 128], BF16)
        onesb = sb("onesb", [128, 128], BF16)
        mbase = sb("mbase", [128, 2048], BF16)
        cosT = sb("cosT", [128, NT, 8], F32)
        sinT = sb("sinT", [128, NT, 8], F32)
        gmix = sb("gmix", [128, KC], F32)
        gple = sb("gple", [128, KC], F32)
        subg = sb("subg", [128, 1], F32)
        lam = sb("lam", [128, 4], F32)
        mixedT = sb("mixedT", [128, NH, S], BF16)

        def mask_strict(k):
            o = 384 - 128 * k
            return mbase[:, o:o + 512]

        def mask_incl(k):
            o = 1024 + 384 - 128 * k
            return mbase[:, o:o + 512]

        with ExitStack() as s12:
            hnT = sb("hnT", [128, KC, S], BF16, s12)

            with ExitStack() as s1:
                P = Prog(nc, "b1")
                sems = SemPool(nc, top, "b1")
                c32 = sb("c32", [128, 2048], F32, s1)
                c32b = sb("c32b", [128, 128], F32, s1)
                c32c = sb("c32c", [128, 128], F32, s1)
                lamv = sb("lamv", [128, 4, 64], F32, s1)
                lamp = sb("lamp", [128, 2, 64], F32, s1)
                lams = sb("lams", [128, 4], F32, s1)
                xt = [sb(f"xt{i}", [128, D], F32, s1) for i in range(2)]
                xs = [sb(f"xs{i}", [128, D], BF16, s1) for i in range(2)]
                junk = sb("junk", [128, D], BF16, s1)
                stats = sb("stats", [128, NT, 4], F32, s1)

                P.dma("sp", lambda e: e.dma_start(out=c32[:], in_=mbase_d[:, :]), "c0_1", w=["c32"])
                P.dma("sp", lambda e: e.dma_start(out=c32b[:], in_=ident_d[:, :]), "c0_2", w=["c32b"])
                P.dma("sp", lambda e: e.dma_start(out=c32c[:], in_=ntri_d[:, :]), "c0_3", w=["c32c"])
                P.dma("sp", lambda e: e.dma_start(out=cosT[:], in_=cos_d[:, :, :]), "c0_4", w=["cos"])
                P.dma("sp", lambda e: e.dma_start(out=sinT[:], in_=sin_d[:, :, :]), "c0_5", w=["sin"])
                P.dma("sp", lambda e: e.dma_start(out=gmix[:], in_=gmix_d[:, :]), "c0_6", w=["gmix"])
                P.dma("sp", lambda e: e.dma_start(out=gple[:], in_=gple_d[:, :]), "c0_7", w=["gple"])
                P.dma("sp", lambda e: e.dma_start(out=subg[:], in_=subg_d[:, :]), "c0_8", w=["subg"])
                P.dma("sp", lambda e: e.dma_start(out=lamv[:], in_=lamv_d[:, :, :]), "c0_9", w=["lamv"])
                P.pool(lambda e: e.tensor_copy(out=mbase[:], in_=c32[:]), r=["c32"], w=["mbase"])
                P.pool(lambda e: e.tensor_copy(out=idb[:], in_=c32b[:]), r=["c32b"], w=["idb"])
                P.pool(lambda e: e.tensor_copy(out=ntrib[:], in_=c32c[:]), r=["c32c"], w=["ntrib"])
                P.pool(lambda e: e.memset(nonesb[:], -1.0), w=["nonesb"])
                P.pool(lambda e: e.memset(onesb[:], 1.0), w=["onesb"])
                P.dve(lambda e: e.tensor_tensor(out=lamp[:, 0, :], in0=lamv[:, 0, :], in1=lamv[:, 1, :], op=ALU.mult),
                      r=["lamv"], w=["lamp0"])
                P.dve(lambda e: e.tensor_tensor(out=lamp[:, 1, :], in0=lamv[:, 2, :], in1=lamv[:, 3, :], op=ALU.mult),
                      r=["lamv"], w=["lamp1"])
                P.dve(lambda e: e.tensor_reduce(out=lams[:, 0:2], in_=lamp[:], axis=AX.X, op=ALU.add),
                      r=["lamp0", "lamp1"], w=["lams01"])
                P.act(lambda e: e.activation(out=lams[:, 2:4], in_=lams[:, 0:2], func=AF.Exp),
                      r=["lams01"], w=["lams23"])
                P.dve(lambda e: e.tensor_tensor(out=lam[:, 2:3], in0=lams[:, 2:3], in1=lams[:, 3:4], op=ALU.subtract),
                      r=["lams23"], w=["lam2"])
                P.dve(lambda e: e.tensor_scalar(out=lam[:, 0:1], in0=lam[:, 2:3], scalar1=LAMBDA_INIT, scalar2=None,
                                                op0=ALU.add), r=["lam2"], w=["lam0"])
                P.dve(lambda e: e.tensor_scalar(out=lam[:, 1:2], in0=lam[:, 0:1], scalar1=-1.0, scalar2=None,
                                                op0=ALU.mult), r=["lam0"], w=["lam"])

                for i in range(NT):
                    b = i % 2
                    P.dma("sp", lambda e, i=i, b=b: e.dma_start(out=xt[b][:], in_=x_d[i * 128:(i + 1) * 128, :]),
                          f"x{b}", w=[f"xt{b}"])
                    P.act(lambda e, i=i, b=b: e.activation(out=junk[:], in_=xt[b][:], func=AF.Square,
                                                           accum_out=stats[:, i, 0:1]),
                          r=[f"xt{b}"], w=["junk", f"ss{i}"])
                    P.act(lambda e, i=i: e.activation(out=stats[:, i, 1:2], in_=stats[:, i, 0:1], func=AF.Ln,
                                                      scale=1.0 / D, bias=NORM_EPS), r=[f"ss{i}"], w=[f"ln{i}"])
                    P.act(lambda e, i=i: e.activation(out=stats[:, i, 2:3], in_=stats[:, i, 1:2], func=AF.Exp,
                                                      scale=-0.5), r=[f"ln{i}"], w=[f"rs{i}"])
                    P.dve(lambda e, i=i, b=b: e.tensor_scalar(out=xs[b][:], in0=xt[b][:], scalar1=stats[:, i, 2:3],
                                                              scalar2=None, op0=ALU.mult),
                          r=[f"xt{b}", f"rs{i}"], w=[f"xs{b}"])
                    for q4 in range(4):
                        bank = (4 * i + q4) % 8
                        pT = pb[bank][:].rearrange("p (a c) -> p a c", a=4)

                        def tr(e, b=b, q4=q4, pT=pT):
                            for j in range(4):
                                kc = q4 * 4 + j
                                ins = e.matmul(pT[:, j, :], lhsT=xs[b][:, kc * 128:(kc + 1) * 128], rhs=idb[:],
                                               start=True, stop=True)
                            return ins
                        P.pe(tr, r=[f"xs{b}", "idb"], w=[BK(bank)])
                        gb = bass.AP(gmix.tensor if hasattr(gmix, "tensor") else gmix, q4 * 4,
                                     [[KC, 128], [1, 4], [0, 128]])
                        P.dve(lambda e, i=i, q4=q4, pT=pT, gb=gb: e.tensor_tensor(
                            out=hnT[:, q4 * 4:(q4 + 1) * 4, i * 128:(i + 1) * 128], in0=pT, in1=gb, op=ALU.mult),
                            r=[BK(bank), "gmix"], w=[("hnT", i, q4)])
                if dbg_hn:
                    for kc in range(KC):
                        P.dve(lambda e, kc=kc: e.tensor_copy(out=xt[0][:, 0:S], in_=hnT[:, kc, :]),
                              r=[("hnT", i, q4) for i in range(NT) for q4 in range(4)] + ["xt0"], w=["xt0"])
                        P.dma("sp", lambda e, kc=kc: e.dma_start(out=dbg_d[:, kc, :], in_=xt[0][:, 0:S]), "dbg", r=["xt0"])
                P.emit(sems)
            nc.all_engine_barrier()
            if nblocks < 2:
                return nc

            with ExitStack() as s2:
                P = Prog(nc, "b2")
                sems = SemPool(nc, top, "b2")
                wst = [sb(f"wst{i}", [128, KC, 128], F32, s2) for i in range(2)]
                wbf = [sb(f"wbf{i}", [128, KC, 128], BF16, s2) for i in range(2)]
                qT = sb("qT", [128, S], BF16, s2)
                kT = sb("kT", [128, S], BF16, s2)
                kT1 = sb("kT1", [128, S], BF16, s2)
                sgT = sb("sgT", [128, S], BF16, s2)
                vtok = sb("vtok", [128, NT, 128], BF16, s2)
                tokbf = sb("tokbf", [128, NT, 128], BF16, s2)
                xr = sb("xr", [128, NT, 2, 16], F32, s2)
                rt = [sb(f"rt{i}", [128, NT, 2, 8], F32, s2) for i in range(4)]
                e32 = [sb(f"e32_{i}", [128, 512], F32, s2) for i in range(2)]
                spb = [sb(f"spb{i}", [128, 512], BF16, s2) for i in range(2)]
                wTb = [sb(f"wTb{i}", [128, 512], BF16, s2) for i in range(2)]
                ls32 = sb("ls32", [128, 512], F32, s2)
                lsb = [sb(f"lsb{i}", [128, 512], BF16, s2) for i in range(2)]
                Eb = [spb, wTb]
                EbK = [["spb0", "spb1"], ["wTb0", "wTb1"]]
                ep = [e32[0], e32[1]] + [sb(f"ep{i}", [128, 512], F32, s2) for i in range(2, 4)]
                epK = ["e32_0", "e32_1", "ep2", "ep3"]

                for i_ in range(2):
                    P.pool(lambda e, i_=i_: e.memset(e32[i_][:], 0.0), w=[f"e32_{i_}"])
                    P.pool(lambda e, i_=i_: e.memset(spb[i_][:], 0.0), w=[f"spb{i_}"])
                    P.pool(lambda e, i_=i_: e.memset(wTb[i_][:], 0.0), w=[f"wTb{i_}"])

                wstate = {"n": 0}

                def load_wtile(col0):
                    n = wstate["n"]
                    wstate["n"] += 1
                    b = n % 2
                    P.dma("sp", lambda e: e.dma_start(out=wst[b][:], in_=win_v[:, :, col0:col0 + 128]),
                          f"w{b}", w=[f"wst{b}"])
                    castf = P.dve if (n % 2 == 0) else P.pool
                    castf(lambda e: e.tensor_copy(out=wbf[b][:], in_=wst[b][:]), r=[f"wst{b}"], w=[f"wbf{b}"])
                    return b

                def precast(idx):
                    mi, q = idx // 16, idx % 16
                    wv = wout_v if mi == 0 else wgate_v
                    n = wstate["n"]
                    wstate["n"] += 1
                    b = n % 2
                    P.dma("sp", lambda e: e.dma_start(out=wst[b][:], in_=wv[:, :, q * 128:(q + 1) * 128]),
                          f"w{b}", w=[f"wst{b}"])
                    P.pool(lambda e: e.tensor_copy(out=wbf[b][:], in_=wst[b][:]), r=[f"wst{b}"], w=[f"wbf{b}"])
                    dst = wsc_d[mi, q // 2].rearrange("p (kc c) -> p kc c", c=256)[:, :, (q % 2) * 128:(q % 2 + 1) * 128]
                    P.dma("sp", lambda e: e.dma_start(out=dst, in_=wbf[b][:]), f"wo{b}", r=[f"wbf{b}"])

                bankrr = {"n": 0}

                def next_bank():
                    bankrr["n"] += 1
                    return bankrr["n"] % 8

                def proj_feature(col0, evac):
                    b = load_wtile(col0)
                    for tt in range(NTT):
                        bank = next_bank()

                        def mm(e, b=b, tt=tt, bank=bank):
                            for kc in range(KC):
                                ins = e.matmul(pb[bank][:], lhsT=wbf[b][:, kc, :],
                                               rhs=hnT[:, kc, tt * 512:(tt + 1) * 512],
                                               start=(kc == 0), stop=(kc == KC - 1))
                            return ins
                        P.pe(mm, r=[f"wbf{b}"] + [("hnT", i, hf) for i in range(tt * 4, tt * 4 + 4) for hf in range(2)],
                             w=[BK(bank)])
                        evac(tt, bank)

                def proj_token(col0, evac):
                    b = load_wtile(col0)
                    for tt in range(NTT):
                        bank = next_bank()
                        pv = pb[bank][:].rearrange("p (a c) -> p a c", a=4)

                        def mm(e, b=b, tt=tt, pv=pv):
                            for j in range(4):
                                i = tt * 4 + j
                                for kc in range(KC):
                                    ins = e.matmul(pv[:, j, :], lhsT=hnT[:, kc, i * 128:(i + 1) * 128],
                                                   rhs=wbf[b][:, kc, :], start=(kc == 0), stop=(kc == KC - 1))
                            return ins
                        P.pe(mm, r=[f"wbf{b}"] + [("hnT", i, hf) for i in range(tt * 4, tt * 4 + 4) for hf in range(2)],
                             w=[BK(bank)])
                        evac(tt, bank, pv)

                def ev_q_sb(tt, bank):
                    P.act(lambda e: e.activation(out=qT[:, tt * 512:(tt + 1) * 512], in_=pb[bank][:], func=AF.Copy,
                                                 scale=SB_SCALE), r=[BK(bank)], w=[("qT", tt)])

                def ev_k(tt, bank):
                    P.dve(lambda e: e.tensor_copy(out=kT[:, tt * 512:(tt + 1) * 512], in_=pb[bank][:]),
                          r=[BK(bank)], w=[("kT", tt)])

                def ev_g(tt, bank):
                    P.act(lambda e: e.activation(out=sgT[:, tt * 512:(tt + 1) * 512], in_=pb[bank][:], func=AF.Silu),
                          r=[BK(bank)], w=[("sgT", tt)])

                def ev_v(tt, bank, pv):
                    P.dve(lambda e: e.tensor_copy(out=vtok[:, tt * 4:(tt + 1) * 4, :], in_=pv),
                          r=[BK(bank)], w=[("vtok", tt)])

                def ev_tok(tt, bank, pv):
                    P.dve(lambda e: e.tensor_copy(out=tokbf[:, tt * 4:(tt + 1) * 4, :], in_=pv),
                          r=[BK(bank)], w=[("tokbf", tt)])
                    src = pv.rearrange("p a (m d) -> p a m d", m=2)[:, :, :, 0:16]
                    P.dve(lambda e: e.tensor_copy(out=xr[:, tt * 4:(tt + 1) * 4, :, :], in_=src),
                          r=[BK(bank)], w=[("xr", tt)])

                def rope_and_transpose(dstT, dstname, split=False):
                    x1 = xr[:, :, :, 0:8]
                    x2 = xr[:, :, :, 8:16]
                    cb = bass.AP(cosT.tensor if hasattr(cosT, "tensor") else cosT, 0,
                                 [[NT * 8, 128], [8, NT], [0, 2], [1, 8]])
                    sbp = bass.AP(sinT.tensor if hasattr(sinT, "tensor") else sinT, 0,
                                  [[NT * 8, 128], [8, NT], [0, 2], [1, 8]])
                    xrk = [("xr", tt) for tt in range(NTT)]
                    tk = [("tokbf", tt) for tt in range(NTT)]
                    P.dve(lambda e: e.tensor_tensor(out=rt[0][:], in0=x1, in1=cb, op=ALU.mult), r=xrk + ["cos"], w=["rt0"])
                    P.dve(lambda e: e.tensor_tensor(out=rt[1][:], in0=x2, in1=sbp, op=ALU.mult), r=xrk + ["sin"], w=["rt1"])
                    P.dve(lambda e: e.tensor_tensor(out=rt[2][:], in0=x2, in1=cb, op=ALU.mult), r=xrk + ["cos"], w=["rt2"])
                    P.dve(lambda e: e.tensor_tensor(out=rt[3][:], in0=x1, in1=sbp, op=ALU.mult), r=xrk + ["sin"], w=["rt3"])
                    tv = tokbf[:].rearrange("p a (m d) -> p a m d", m=2)
                    P.dve(lambda e: e.tensor_tensor(out=tv[:, :, :, 0:8], in0=rt[0][:], in1=rt[1][:], op=ALU.subtract),
                          r=["rt0", "rt1"] + tk, w=tk)
                    P.dve(lambda e: e.tensor_tensor(out=tv[:, :, :, 8:16], in0=rt[2][:], in1=rt[3][:], op=ALU.add),
                          r=["rt2", "rt3"] + tk, w=tk)
                    for tt in range(NTT):
                        bank = next_bank()
                        pT = pb[bank][:].rearrange("p (a c) -> p a c", a=4)

                        def tr(e, tt=tt, pT=pT):
                            for j in range(4):
                                ins = e.matmul(pT[:, j, :], lhsT=tokbf[:, tt * 4 + j, :], rhs=idb[:], start=True, stop=True)
                            return ins
                        P.pe(tr, r=tk + ["idb"], w=[BK(bank)])
                        if not split:
                            P.act(lambda e, tt=tt, bank=bank: e.activation(
                                out=dstT[:, tt * 512:(tt + 1) * 512], in_=pb[bank][:], func=AF.Copy),
                                r=[BK(bank)], w=[(dstname, tt)])
                        else:
                            P.act(lambda e, tt=tt, bank=bank: e.activation(
                                out=kT[0:64, tt * 512:(tt + 1) * 512], in_=pb[bank][0:64, :], func=AF.Copy),
                                r=[BK(bank)], w=[("kT", tt)])
                            P.act(lambda e, tt=tt, bank=bank: e.activation(
                                out=kT1[64:128, tt * 512:(tt + 1) * 512], in_=pb[bank][64:128, :], func=AF.Copy),
                                r=[BK(bank)], w=[("kT1", tt)])

                def sb_attention(h):
                    pairs = []
                    for T in range(NTT):
                        for c in range(4 * T + 3, -1, -1):
                            pairs.append((T, c))
                    zb = [next_bank(), next_bank()]
                    bb = [next_bank(), next_bank()]
                    ob = [next_bank(), next_bank()]
                    npair = len(pairs)

                    def emit_z(n):
                        T, c = pairs[n]
                        bank = zb[n % 2]
                        P.pe(lambda e: e.matmul(pb[bank][:], lhsT=kT[:, c * 128:(c + 1) * 128],
                                                rhs=qT[:, T * 512:(T + 1) * 512], start=True, stop=True),
                             r=[("kT", c // 4), ("qT", T)], w=[BK(bank)])

                    def emit_w(n):
                        T, c = pairs[n]
                        k = c - 4 * T
                        first = (c == 4 * T + 3)
                        last = (c == 0)
                        c0w = 128 * k if k > 0 else 0
                        P.act(lambda e: e.activation(out=wTb[n % 2][:, c0w:512], in_=pb[bb[n % 2]][:, c0w:512], func=AF.Exp),
                              r=[BK(bb[n % 2])], w=[f"wTb{n % 2}"])
                        if k >= 0:
                            P.dve(lambda e: e.tensor_tensor(out=wTb[n % 2][:], in0=wTb[n % 2][:], in1=mask_strict(k),
                                                             op=ALU.mult), r=[f"wTb{n % 2}", "mbase"], w=[f"wTb{n % 2}"])
                        P.pe(lambda e: e.matmul(pb[ob[T % 2]][:], lhsT=vtok[:, c, :], rhs=wTb[n % 2][:],
                                                start=first, stop=last),
                             r=[("vtok", c // 4), f"wTb{n % 2}"], w=[BK(ob[T % 2])], defer=(not last))
                        if last:
                            P.dve(lambda e: e.tensor_tensor(out=mixedT[:, h, T * 512:(T + 1) * 512], in0=pb[ob[T % 2]][:],
                                                            in1=sgT[:, T * 512:(T + 1) * 512], op=ALU.mult),
                                  r=[BK(ob[T % 2]), ("sgT", T)], w=[("mixedT", h, T)])

                    def col0(n):
                        T_, c_ = pairs[n]
                        k_ = c_ - 4 * T_
                        return 128 * k_ if k_ > 0 else 0

                    def emit_e(n):
                        zbank = zb[n % 2]
                        i_ = n % 2
                        c0 = col0(n)
                        P.act(lambda e: e.activation(out=e32[i_][:, c0:512], in_=pb[zbank][:, c0:512], func=AF.Exp),
                              r=[BK(zbank)], w=[f"e32_{i_}"])

                    emit_z(0)
                    if npair > 1:
                        emit_z(1)
                    emit_e(0)
                    for n in range(npair):
                        T, c = pairs[n]
                        k = c - 4 * T
                        first = (c == 4 * T + 3)
                        if n + 2 < npair:
                            emit_z(n + 2)
                        if n + 1 < npair:
                            emit_e(n + 1)
                        sbuf_i = n % 2
                        c0n = col0(n)
                        P.act(lambda e, sbuf_i=sbuf_i, c0n=c0n: e.activation(out=spb[sbuf_i][:, c0n:512], in_=e32[sbuf_i][:, c0n:512],
                                                                            func=AF.Ln, bias=1.0),
                              r=[f"e32_{sbuf_i}"], w=[f"spb{sbuf_i}"])
                        if k >= 0:
                            P.dve(lambda e, sbuf_i=sbuf_i, k=k: e.tensor_tensor(out=spb[sbuf_i][:], in0=spb[sbuf_i][:],
                                                                                in1=mask_strict(k), op=ALU.mult),
                                  r=[f"spb{sbuf_i}", "mbase"], w=[f"spb{sbuf_i}"])
                        m = (4 * T + 3) - c
                        bbank = bb[n % 2]
                        if m == 0:
                            lsrc, lkey = None, None
                        elif m == 1:
                            lsrc, lkey = spb[(n - 1) % 2], f"spb{(n - 1) % 2}"
                        else:
                            lsrc, lkey = lsb[m % 2], f"lsb{m % 2}"

                        def bmm(e, T=T, c=c, sbuf_i=sbuf_i, bbank=bbank, lsrc=lsrc):
                            e.matmul(pb[bbank][:], lhsT=kT[:, c * 128:(c + 1) * 128],
                                     rhs=qT[:, T * 512:(T + 1) * 512], start=True, stop=False)
                            ins = e.matmul(pb[bbank][:], lhsT=ntrib[:], rhs=spb[sbuf_i][:], start=False,
                                           stop=(lsrc is None))
                            if lsrc is not None:
                                ins = e.matmul(pb[bbank][:], lhsT=nonesb[:], rhs=lsrc[:], start=False, stop=True)
                            return ins
                        if sbstage >= 6:
                            P.pe(bmm, r=[("kT", c // 4), ("qT", T), "ntrib", "nonesb", f"spb{sbuf_i}"] + ([lkey] if lkey else []),
                                 w=[BK(bbank)])
                        if c > 0 and m >= 1 and sbstage >= 7:
                            if m == 1:
                                P.dve(lambda e, n=n: e.tensor_tensor(out=ls32[:], in0=spb[(n - 1) % 2][:], in1=spb[n % 2][:],
                                                                     op=ALU.add),
                                      r=[f"spb{(n - 1) % 2}", f"spb{n % 2}"], w=["ls32"])
                            else:
                                P.dve(lambda e, n=n: e.tensor_tensor(out=ls32[:], in0=ls32[:], in1=spb[n % 2][:], op=ALU.add),
                                      r=["ls32", f"spb{n % 2}"], w=["ls32"])
                            P.dve(lambda e, m=m: e.tensor_copy(out=lsb[(m + 1) % 2][:], in_=ls32[:]),
                                  r=["ls32"], w=[f"lsb{(m + 1) % 2}"])
                        if n >= 1 and sbstage >= 8:
                            emit_w(n - 1)
                    if sbstage >= 8:
                        emit_w(npair - 1)

                def df_attention(h):
                    pairs = []
                    for T in range(NTT):
                        for c in range(0, 4 * T + 4):
                            pairs.append((T, c))
                    npair = len(pairs)
                    sbk = [[next_bank(), next_bank()] for _ in range(2)]
                    obk = [next_bank(), next_bank()]
                    zbk = [next_bank(), next_bank()]

                    def emit_s(n):
                        T, c = pairs[n]
                        for m in range(2):
                            bank = sbk[m][n % 2]
                            P.pe(lambda e, m=m, bank=bank: e.matmul(
                                pb[bank][:], lhsT=(kT if m == 0 else kT1)[:, c * 128:(c + 1) * 128],
                                rhs=qT[:, T * 512:(T + 1) * 512], start=True, stop=True),
                                r=[("kT", c // 4), ("kT1", c // 4), ("qT", T)], w=[BK(bank)])

                    emit_s(0)
                    pending = None
                    for n in range(npair):
                        T, c = pairs[n]
                        k = c - 4 * T
                        first = (c == 0)
                        last = (c == 4 * T + 3)
                        if n + 1 < npair:
                            emit_s(n + 1)
                        for m in range(2):
                            bank = sbk[m][n % 2]
                            c0d = 128 * k if k > 0 else 0
                            P.act(lambda e, m=m, bank=bank, n=n, c0d=c0d: e.activation(out=Eb[m][n % 2][:, c0d:512],
                                                                                       in_=pb[bank][:, c0d:512],
                                                                                       func=AF.Exp, scale=DF_SCALE),
                                  r=[BK(bank)], w=[EbK[m][n % 2]])
                            if k >= 0:
                                mf = P.dve
                                mf(lambda e, m=m, n=n, k=k: e.tensor_tensor(out=Eb[m][n % 2][:], in0=Eb[m][n % 2][:],
                                                                            in1=mask_incl(k), op=ALU.mult),
                                   r=[EbK[m][n % 2], "mbase"], w=[EbK[m][n % 2]])
                        for m in (range(2) if dfstage >= 6 else []):
                            P.pe(lambda e, m=m, n=n, c=c, first=first, last=last: e.matmul(
                                pb[obk[m]][:], lhsT=vtok[:, c, :], rhs=Eb[m][n % 2][:], start=first, stop=last),
                                r=[("vtok", c // 4), EbK[m][n % 2]], w=[BK(obk[m])], defer=(not last))
                            P.pe(lambda e, m=m, n=n, first=first, last=last: e.matmul(
                                pb[zbk[m]][:], lhsT=onesb[:], rhs=Eb[m][n % 2][:], start=first, stop=last),
                                r=["onesb", EbK[m][n % 2]], w=[BK(zbk[m])], defer=(not last))
                        if pending is not None:
                            pending(sbk[0][n % 2])
                            pending = None
                        if last and dfstage >= 7:
                            for m in range(2):
                                P.act(lambda e, m=m: e.activation(out=ep[m][:], in_=pb[obk[m]][:], func=AF.Copy),
                                      r=[BK(obk[m])], w=[epK[m]])
                                P.dve(lambda e, m=m: e.reciprocal(out=ep[2 + m][:], in_=pb[zbk[m]][:]),
                                      r=[BK(zbk[m])], w=[epK[2 + m]])

                            def part2(msbank, T=T):
                                ts = slice(T * 512, (T + 1) * 512)
                                for m in range(2):
                                    P.dve(lambda e, m=m: e.tensor_tensor(out=ep[m][:], in0=ep[m][:], in1=ep[2 + m][:], op=ALU.mult),
                                          r=[epK[m], epK[2 + m]], w=[epK[m]])
                                P.dve(lambda e: e.scalar_tensor_tensor(out=ep[2][:], in0=ep[1][:], scalar=lam[:, 1:2], in1=ep[0][:],
                                                                       op0=ALU.mult, op1=ALU.add),
                                      r=[epK[0], epK[1], "lam"], w=["ep2"])
                                d2b = lsb[0]
                                P.dve(lambda e: e.tensor_tensor(out=d2b[:], in0=ep[2][:], in1=ep[2][:], op=ALU.mult),
                                      r=["ep2"], w=["lsb0"])
                                P.pe(lambda e: e.matmul(pb[msbank][:], lhsT=onesb[:], rhs=d2b[:], start=True, stop=True),
                                     r=["onesb", "lsb0"], w=[BK(msbank)])
                                P.act(lambda e: e.activation(out=ep[0][:], in_=pb[msbank][:], func=AF.Ln, scale=1.0 / 128.0,
                                                             bias=SUBLN_EPS),
                                      r=[BK(msbank)], w=[epK[0]])
                                P.act(lambda e: e.activation(out=ep[1][:], in_=ep[0][:], func=AF.Exp, scale=-0.5,
                                                             bias=math.log(1.0 - LAMBDA_INIT)),
                                      r=[epK[0]], w=[epK[1]])
                                P.dve(lambda e: e.tensor_tensor(out=ep[2][:], in0=ep[2][:], in1=ep[1][:], op=ALU.mult),
                                      r=["ep2", epK[1]], w=["ep2"])
                                P.dve(lambda e, ts=ts: e.scalar_tensor_tensor(out=mixedT[:, n_sb_off + h, ts], in0=ep[2][:],
                                                                              scalar=subg[:, 0:1], in1=sgT[:, ts],
                                                                              op0=ALU.mult, op1=ALU.mult),
                                      r=["ep2", "subg", ("sgT", T)], w=[("mixedT", n_sb_off + h, T)])
                            if n == npair - 1:
                                part2(sbk[0][n % 2])
                            else:
                                pending = part2

                n_sb_off = 8
                PH = {"P": P}

                def new_block(tag):
                    return

                for h in range(n_sb):
                    proj_feature(0 * 1024 + h * 128, ev_q_sb)
                    if stage >= 2:
                        proj_feature(1 * 1024 + h * 128, ev_k)
                        proj_token(2 * 1024 + h * 128, ev_v)
                    if stage >= 3:
                        proj_feature(3 * 1024 + h * 128, ev_g)
                    if nblocks >= 3:
                        precast(2 * h)
                        precast(2 * h + 1)
                    if stage >= 4:
                        sb_attention(h)
                    new_block(f"sb{h}")
                if n_df > 0:
                    P.pool(lambda e: e.memset(kT[64:128, :], 0.0), w=[("kT", tt) for tt in range(NTT)])
                    P.pool(lambda e: e.memset(kT1[0:64, :], 0.0), w=[("kT1", tt) for tt in range(NTT)])
                for h in range(n_df):
                    proj_token(4 * 1024 + h * 128, ev_tok)
                    if stage >= 2:
                        rope_and_transpose(qT, "qT")
                        proj_token(5 * 1024 + h * 128, ev_tok)
                        rope_and_transpose(kT, "kT", split=True)
                    if stage >= 3:
                        proj_token(6 * 1024 + h * 128, ev_v)
                        proj_feature(7 * 1024 + h * 128, ev_g)
                    new_block(f"dfp{h}")
                    if nblocks >= 3:
                        precast(16 + 2 * h)
                        precast(16 + 2 * h + 1)
                    if stage >= 4:
                        df_attention(h)
                    new_block(f"dfa{h}")
                if debug_mixed:
                    dbg32 = sb("dbg32", [128, S], F32, s2)
                    for hc in (list(range(n_sb)) + list(range(8, 8 + n_df))) if stage >= 4 else []:
                        P.dve(lambda e, hc=hc: e.tensor_copy(out=dbg32[:], in_=mixedT[:, hc, :]),
                              r=[("mixedT", hc, T) for T in range(NTT)], w=["dbg32"])
                        P.dma("sp", lambda e, hc=hc: e.dma_start(out=dbg_d[:, hc, :], in_=dbg32[:]), "dbg",
                              r=["dbg32"])
                P.emit(sems)
            nc.all_engine_barrier()

        if nblocks < 3:
            return nc
        with ExitStack() as s3:
            P = Prog(nc, "b3")
            sems = SemPool(nc, top, "b3")
            WC = 256
            NWC = D // WC
            wbf = [sb(f"wbf3_{i}", [128, KC, WC], BF16, s3) for i in range(2)]
            wpj32 = sb("wpj32", [128, 2, 512], F32, s3)
            wpj = sb("wpj", [128, 2, D], BF16, s3)
            gfin = sb("gfin", [128, D], F32, s3)
            hh = sb("hh", [128, 4, D], F32, s3)
            xt = [sb(f"xt3_{i}", [128, D], F32, s3) for i in range(2)]
            hs = sb("hs", [128, D], BF16, s3)
            hn2T = sb("hn2T", [128, KC, 512], BF16, s3)
            pt32 = sb("pt32", [128, PLE], F32, s3)
            ptb = sb("ptb", [128, PLE], BF16, s3)
            pT = sb("pT", [128, 2, 512], BF16, s3)
            gt = [sb(f"gt{i}", [128, WC], F32, s3) for i in range(2)]
            junk = sb("junk3", [128, D], BF16, s3)
            stats = sb("stats3", [128, NT, 8], F32, s3)

            bankrr = {"n": 0}

            def next_bank():
                bankrr["n"] += 1
                return bankrr["n"] % 8

            wstate = {"n": 0, "t": 0}

            def load_wtile3(wv, col0):
                t = wstate["t"]
                wstate["t"] += 1
                b = t % 2
                mi = 0 if wv is wout_v else 1
                src = wsc_d[mi, col0 // WC].rearrange("p (kc c) -> p kc c", c=256)
                P.dma("sp", lambda e: e.dma_start(out=wbf[b][:], in_=src), f"w3_{b}",
                      w=[(f"wbf{b}", q) for q in range(WC // 128)])
                return b

            P.dma("sp", lambda e: e.dma_start(out=gfin[:], in_=gfin_d[:, :]), "c3", w=["gfin"])
            for q in range(D // 512):
                P.dma("sp", lambda e, q=q: e.dma_start(out=wpj32[:], in_=wproj_v[:, :, q * 512:(q + 1) * 512]), "wpj",
                      w=["wpj32"])
                P.dve(lambda e, q=q: e.tensor_copy(out=wpj[:, :, q * 512:(q + 1) * 512], in_=wpj32[:]),
                      r=["wpj32"], w=[("wpj", q)])
            xn = {"n": 0}

            for T in range(NTT):
                for wc in range(NWC):
                    b = load_wtile3(wout_v, wc * WC)
                    for j in range(4):
                        i = T * 4 + j
                        bank = next_bank()

                        def mm(e, b=b, i=i, bank=bank):
                            for kc in range(KC):
                                ins = e.matmul(pb[bank][:, 0:WC], lhsT=mixedT[:, kc, i * 128:(i + 1) * 128],
                                               rhs=wbf[b][:, kc, :], start=(kc == 0), stop=(kc == KC - 1))
                            return ins
                        P.pe(mm, r=[(f"wbf{b}", q) for q in range(WC // 128)], w=[BK(bank)])
                        P.act(lambda e, j=j, wc=wc, bank=bank: e.activation(out=hh[:, j, wc * WC:(wc + 1) * WC],
                                                                            in_=pb[bank][:, 0:WC], func=AF.Copy),
                              r=[BK(bank)], w=[("hh", j, wc)])
                for j in range(4):
                    i = T * 4 + j
                    xb = xn["n"] % 2
                    xn["n"] += 1
                    P.dma("sp", lambda e, i=i, xb=xb: e.dma_start(out=xt[xb][:], in_=x_d[i * 128:(i + 1) * 128, :]),
                          f"x3_{xb}", w=[f"xt{xb}"])
                    hk = [("hh", j, wc) for wc in range(NWC)]
                    P.dve(lambda e, j=j, xb=xb: e.tensor_tensor(out=hh[:, j, :], in0=hh[:, j, :], in1=xt[xb][:], op=ALU.add),
                          r=hk + [f"xt{xb}"], w=hk)
                    P.act(lambda e, j=j, i=i: e.activation(out=junk[:], in_=hh[:, j, :], func=AF.Square,
                                                           accum_out=stats[:, i, 0:1]), r=hk, w=["junk", ("ss", i)])
                    P.act(lambda e, i=i: e.activation(out=stats[:, i, 1:2], in_=stats[:, i, 0:1], func=AF.Ln,
                                                      scale=1.0 / D, bias=NORM_EPS), r=[("ss", i)], w=[("ln", i)])
                    P.act(lambda e, i=i: e.activation(out=stats[:, i, 2:3], in_=stats[:, i, 1:2], func=AF.Exp,
                                                      scale=-0.5), r=[("ln", i)], w=[("rs", i)])
                    P.dve(lambda e, j=j, i=i: e.tensor_scalar(out=hs[:], in0=hh[:, j, :], scalar1=stats[:, i, 2:3],
                                                              scalar2=None, op0=ALU.mult), r=hk + [("rs", i)], w=["hs"])
                    for q4 in range(4):
                        bank = next_bank()
                        pTr = pb[bank][:].rearrange("p (a c) -> p a c", a=4)

                        def tr(e, q4=q4, pTr=pTr):
                            for jj in range(4):
                                kc = q4 * 4 + jj
                                ins = e.matmul(pTr[:, jj, :], lhsT=hs[:, kc * 128:(kc + 1) * 128], rhs=idb[:],
                                               start=True, stop=True)
                            return ins
                        P.pe(tr, r=["hs", "idb"], w=[BK(bank)])
                        gb = bass.AP(gple.tensor if hasattr(gple, "tensor") else gple, q4 * 4,
                                     [[KC, 128], [1, 4], [0, 128]])
                        P.dve(lambda e, j=j, q4=q4, pTr=pTr, gb=gb: e.tensor_tensor(
                            out=hn2T[:, q4 * 4:(q4 + 1) * 4, j * 128:(j + 1) * 128], in0=pTr, in1=gb, op=ALU.mult),
                            r=[BK(bank), "gple"], w=[("hn2T", j, q4)])
                    P.dma("sp", lambda e, i=i: e.dma_start(out=pt32[:], in_=p_d[i * 128:(i + 1) * 128, :]), "p3",
                          w=["pt32"])
                    P.pool(lambda e: e.tensor_copy(out=ptb[:], in_=pt32[:]), r=["pt32"], w=["ptb"])
                    bank = next_bank()
                    pPr = pb[bank][:, 0:256].rearrange("p (a c) -> p a c", a=2)

                    def trp(e, pPr=pPr):
                        for jj in range(2):
                            ins = e.matmul(pPr[:, jj, :], lhsT=ptb[:, jj * 128:(jj + 1) * 128], rhs=idb[:], start=True, stop=True)
                        return ins
                    P.pe(trp, r=["ptb", "idb"], w=[BK(bank)])
                    P.act(lambda e, j=j, pPr=pPr: e.activation(out=pT[:, :, j * 128:(j + 1) * 128], in_=pPr, func=AF.Copy),
                          r=[BK(bank)], w=[("pT", j)])
                for wc in range(NWC):
                    b = load_wtile3(wgate_v, wc * WC)
                    for j in range(4):
                        i = T * 4 + j
                        bank = next_bank()
                        bank2 = next_bank()

                        def mm(e, b=b, j=j, bank=bank):
                            for kc in range(KC):
                                ins = e.matmul(pb[bank][:, 0:WC], lhsT=hn2T[:, kc, j * 128:(j + 1) * 128],
                                               rhs=wbf[b][:, kc, :], start=(kc == 0), stop=(kc == KC - 1))
                            return ins
                        P.pe(mm, r=[(f"wbf{b}", q) for q in range(WC // 128)] + [("hn2T", j, q4) for q4 in range(4)],
                             w=[BK(bank)])

                        def mm2(e, j=j, wc=wc, bank2=bank2):
                            for kc in range(2):
                                ins = e.matmul(pb[bank2][:, 0:WC], lhsT=pT[:, kc, j * 128:(j + 1) * 128],
                                               rhs=wpj[:, kc, wc * WC:(wc + 1) * WC], start=(kc == 0), stop=(kc == 1))
                            return ins
                        P.pe(mm2, r=[("pT", j)] + [("wpj", q) for q in range(D // 512)], w=[BK(bank2)])
                        g = (wc * 4 + j) % 2
                        P.act(lambda e, g=g, bank=bank: e.activation(out=gt[g][:], in_=pb[bank][:, 0:WC], func=AF.Sigmoid),
                              r=[BK(bank)], w=[f"gt{g}"])
                        P.dve(lambda e, g=g, bank2=bank2: e.tensor_tensor(out=gt[g][:], in0=gt[g][:], in1=pb[bank2][:, 0:WC],
                                                                          op=ALU.mult), r=[f"gt{g}", BK(bank2)], w=[f"gt{g}"])
                        P.dve(lambda e, g=g, j=j, wc=wc: e.tensor_tensor(out=hh[:, j, wc * WC:(wc + 1) * WC],
                                                                         in0=hh[:, j, wc * WC:(wc + 1) * WC], in1=gt[g][:],
                                                                         op=ALU.add),
                              r=[f"gt{g}", ("hh", j, wc)], w=[("hh", j, wc)])
                for j in range(4):
                    i = T * 4 + j
                    hk = [("hh", j, wc) for wc in range(NWC)]
                    P.act(lambda e, j=j, i=i: e.activation(out=junk[:], in_=hh[:, j, :], func=AF.Square,
                                                           accum_out=stats[:, i, 4:5]), r=hk, w=["junk", ("ss2", i)])
                    P.act(lambda e, i=i: e.activation(out=stats[:, i, 5:6], in_=stats[:, i, 4:5], func=AF.Ln,
                                                      scale=1.0 / D, bias=NORM_EPS), r=[("ss2", i)], w=[("ln2", i)])
                    P.act(lambda e, i=i: e.activation(out=stats[:, i, 6:7], in_=stats[:, i, 5:6], func=AF.Exp,
                                                      scale=-0.5), r=[("ln2", i)], w=[("rs2", i)])
                    xb = xn["n"] % 2
                    xn["n"] += 1
                    P.dve(lambda e, j=j, i=i, xb=xb: e.scalar_tensor_tensor(out=xt[xb][:], in0=hh[:, j, :],
                                                                            scalar=stats[:, i, 6:7], in1=gfin[:],
                                                                            op0=ALU.mult, op1=ALU.mult),
                          r=hk + [("rs2", i), "gfin"], w=[f"xt{xb}"])
                    P.dma("sp", lambda e, i=i, xb=xb: e.dma_start(out=y_d[i * 128:(i + 1) * 128, :], in_=xt[xb][:]),
                          f"x3_{xb}", r=[f"xt{xb}"])
            P.emit(sems)
    return nc


def make_consts(S):
    NT = S // 128
    half = 8
    inv_freq = (500000.0 ** (-np.arange(0, 16, 2, dtype=np.float32) / np.float32(16))).astype(np.float32)
    pos = np.arange(S, dtype=np.float32)
    ang = (pos[:, None] * inv_freq[None, :]).astype(np.float32)
    cos = np.cos(ang).astype(np.float32).reshape(NT, 128, half).transpose(1, 0, 2)
    sin = np.sin(ang).astype(np.float32).reshape(NT, 128, half).transpose(1, 0, 2)
    ident = np.eye(128, dtype=np.float32)
    j = np.arange(128)[:, None]
    s = np.arange(128)[None, :]
    ntri = np.where(j >= s, -1.0, 0.0).astype(np.float32)
    u = np.arange(1024)[None, :]
    mbase = np.concatenate([np.where(j < u - 384, 1.0, 0.0), np.where(j <= u - 384, 1.0, 0.0)], axis=1).astype(np.float32)
    return {
        "cos": np.ascontiguousarray(cos), "sin": np.ascontiguousarray(sin),
        "ident": ident, "ntri": ntri, "mbase": mbase,
    }


_NC_CACHE = {}


def kernel(x, p, norm_mix_g, w_in, lambda_q1, lambda_k1, lambda_q2, lambda_k2, subln_g,
           w_out, norm_ple_g, w_ple_gate, w_ple_proj, norm_final_g):
    x = np.asarray(x, dtype=np.float32)
    p = np.asarray(p, dtype=np.float32)
    B, S, _ = x.shape
    consts = make_consts(S)
    f = lambda a: np.ascontiguousarray(np.asarray(a, dtype=np.float32))
    shared = {
        "w_in": f(w_in[0]), "w_out": f(w_out[0]), "w_gate": f(w_ple_gate[0]), "w_proj": f(w_ple_proj[0]),
        "gmix": f(np.asarray(norm_mix_g[0]).reshape(KC, 128).T),
        "gple": f(np.asarray(norm_ple_g[0]).reshape(KC, 128).T),
        "gfin": f(np.broadcast_to(np.asarray(norm_final_g)[None, :], (128, D))),
        "subg": f(np.asarray(subln_g[0]).reshape(128, 1)),
        "lamv": f(np.broadcast_to(np.stack([np.asarray(lambda_q1[0]), np.asarray(lambda_k1[0]),
                                            np.asarray(lambda_q2[0]), np.asarray(lambda_k2[0])])[None], (128, 4, 64))),
    }
    shared.update(consts)
    if S not in _NC_CACHE:
        _NC_CACHE[S] = build_nc(S)
    nc = _NC_CACHE[S]
    in_maps = []
    for b in range(B):
        m = dict(shared)
        m["x"] = f(x[b])
        m["p"] = f(p[0, b])
        in_maps.append(m)
    res = run_bass_kernel_spmd(nc, in_maps, core_ids=list(range(B)))
    out = np.stack([np.asarray(r["y"]).reshape(S, D) for r in res.results], axis=0)
    return out.astype(np.float32)
```

```python
import math
import numpy as np
import concourse.bass as bass
import concourse.mybir as mybir
from concourse.bass_utils import run_bass_kernel_spmd

F32 = mybir.dt.float32
BF16 = mybir.dt.bfloat16
AF = mybir.ActivationFunctionType
ALU = mybir.AluOpType
AX = mybir.AxisListType

D = 2048
KC = 16
PLE = 256
NORM_EPS = 1e-6
SUBLN_EPS = 1e-5
LAMBDA_INIT = 0.8 - 0.6 * math.exp(-0.3 * 0)
SB_SCALE = 128 ** -0.5
DF_SCALE = 64 ** -0.5


class Op:
    __slots__ = ("eng", "fn", "deps", "raw", "is_dma", "slot", "dcount", "needed", "count", "defer", "seq")


class Prog:
    ENGS = ("pe", "act", "dve", "pool", "sp")

    def __init__(self, nc, name):
        self.nc = nc
        self.name = name
        self.ops = {e: [] for e in self.ENGS}
        self.lastw = {}
        self.readers = {}
        self.slot_tot = {}

    def _add(self, eng, fn, r, w, is_dma=False, slot=None, defer=False):
        o = Op()
        o.eng, o.fn, o.is_dma, o.slot = eng, fn, is_dma, slot
        o.defer = defer
        self.nseq = getattr(self, "nseq", 0) + 1
        o.seq = self.nseq
        o.needed = False
        o.count = 0
        o.dcount = 0
        deps = {}
        for k in r:
            lw = self.lastw.get(k)
            if lw is not None:
                deps[id(lw)] = (lw, True)
        for k in w:
            lw = self.lastw.get(k)
            if lw is not None and id(lw) not in deps:
                deps[id(lw)] = (lw, False)
            for rd in self.readers.get(k, ()):
                if id(rd) not in deps:
                    deps[id(rd)] = (rd, False)
        for k in r:
            self.readers.setdefault(k, []).append(o)
        for k in w:
            self.lastw[k] = o
            self.readers[k] = []
        keep = []
        for d, israw in deps.values():
            if d is o:
                continue
            if d.is_dma or is_dma:
                keep.append(d)
            elif d.eng == eng:
                if eng in ("act", "dve") or (eng == "pool" and israw):
                    keep.append(d)
            else:
                keep.append(d)
        o.deps = keep
        for d in keep:
            d.needed = True
        if is_dma:
            self.slot_tot[slot] = self.slot_tot.get(slot, 0) + 16
            o.dcount = self.slot_tot[slot]
        self.ops[eng].append(o)
        return o

    def pe(self, fn, r=(), w=(), defer=False):
        return self._add("pe", fn, r, w, defer=defer)

    def act(self, fn, r=(), w=()):
        return self._add("act", fn, r, w)

    def dve(self, fn, r=(), w=()):
        return self._add("dve", fn, r, w)

    def pool(self, fn, r=(), w=()):
        return self._add("pool", fn, r, w)

    def dma(self, q, fn, slot, r=(), w=()):
        return self._add(q, fn, r, w, is_dma=True, slot=slot)

    def emit(self, sems):
        nc = self.nc
        final = {}
        pel = self.ops["pe"]
        nxt = {}
        last_sig = None
        for o in reversed(pel):
            if not o.defer:
                last_sig = o
            nxt[id(o)] = last_sig
        for e in self.ENGS:
            for o in self.ops[e]:
                o.needed = False
        for e in self.ENGS:
            for o in self.ops[e]:
                nd = []
                for d in o.deps:
                    if (not d.is_dma) and d.eng == "pe" and d.defer:
                        r_ = nxt[id(d)]
                        assert r_ is not None and r_.seq < o.seq, "deferred PE dependency cannot be satisfied"
                        r_.needed = True
                        d = r_
                    if d not in nd:
                        nd.append(d)
                    d.needed = True
                o.deps = nd
        for e in self.ENGS:
            c = 0
            comp = [o for o in self.ops[e] if not o.is_dma]
            if comp:
                comp[-1].needed = True
            for o in self.ops[e]:
                if o.is_dma:
                    continue
                if o.needed:
                    c += 1
                o.count = c
            final[e] = c
        ops = self.ops
        slot_tot = self.slot_tot

        def run(e, eng):
            waited = {}
            for o in ops[e]:
                for d in o.deps:
                    if d.is_dma:
                        key, val = ("slot", d.slot), d.dcount
                    else:
                        key, val = ("eng", d.eng), d.count
                    if waited.get(key, 0) >= val:
                        continue
                    waited[key] = val
                    s = sems[key]
                    eng.wait_ge(s, val)
                ins = o.fn(eng)
                if o.is_dma:
                    ins.then_inc(sems[("slot", o.slot)], 16)
                elif o.needed:
                    ins.then_inc(sems[("eng", e)], 1)
            for e2 in self.ENGS:
                if final[e2] > 0 and waited.get(("eng", e2), 0) < final[e2]:
                    eng.wait_ge(sems[("eng", e2)], final[e2])
            for sl, tot in slot_tot.items():
                if waited.get(("slot", sl), 0) < tot:
                    eng.wait_ge(sems[("slot", sl)], tot)

        with nc.Block() as block:
            @block.tensor
            def _(eng):
                run("pe", eng)

            @block.scalar
            def _(eng):
                run("act", eng)

            @block.vector
            def _(eng):
                run("dve", eng)

            @block.gpsimd
            def _(eng):
                run("pool", eng)

            @block.sync
            def _(eng):
                run("sp", eng)


class SemPool:
    def __init__(self, nc, stack, prefix):
        self.nc, self.stack, self.prefix = nc, stack, prefix
        self.d = {}

    def __getitem__(self, key):
        if key not in self.d:
            nm = self.prefix + "_" + "_".join(str(k) for k in key)
            self.d[key] = self.stack.enter_context(self.nc.semaphore(nm))
        return self.d[key]


def build_nc(S=2048, n_sb=8, n_df=8, debug_mixed=False, nblocks=3, dbg_hn=False, stage=9, sbstage=9, dfstage=9):
    from contextlib import ExitStack
    NT = S // 128
    NTT = S // 512
    NH = 16
    nc = bass.Bass("TRN2", target_bir_lowering=False)

    def din(name, shape):
        return nc.dram_tensor(name, list(shape), F32, kind="ExternalInput").ap()

    x_d = din("x", [S, D])
    p_d = din("p", [S, PLE])
    win_d = din("w_in", [D, 8192])
    wout_d = din("w_out", [D, D])
    wgate_d = din("w_gate", [D, D])
    wproj_d = din("w_proj", [PLE, D])
    gmix_d = din("gmix", [128, KC])
    gple_d = din("gple", [128, KC])
    gfin_d = din("gfin", [128, D])
    subg_d = din("subg", [128, 1])
    lamv_d = din("lamv", [128, 4, 64])
    cos_d = din("cos", [128, NT, 8])
    sin_d = din("sin", [128, NT, 8])
    ident_d = din("ident", [128, 128])
    ntri_d = din("ntri", [128, 128])
    mbase_d = din("mbase", [128, 2048])
    y_d = nc.dram_tensor("y", [S, D], F32, kind="ExternalOutput").ap()
    if debug_mixed or dbg_hn:
        dbg_d = nc.dram_tensor("dbg", [128, NH, S], F32, kind="ExternalOutput").ap()

    wsc_d = nc.dram_tensor("wsc", [2, 8, 128, KC * 256], BF16, kind="Internal").ap()
    win_v = win_d.rearrange("(kc p) c -> p kc c", p=128)
    wout_v = wout_d.rearrange("(kc p) c -> p kc c", p=128)
    wgate_v = wgate_d.rearrange("(kc p) c -> p kc c", p=128)
    wproj_v = wproj_d.rearrange("(kc p) c -> p kc c", p=128)

    with ExitStack() as top:
        def sb(name, shape, dt, stack=top):
            return stack.enter_context(nc.sbuf_tensor("s_" + name, list(shape), dt))

        pb = [top.enter_context(nc.psum_tensor(f"pb{i}", [128, 512], F32)) for i in range(8)]

        def BK(i):
            return ("bank", i)

        idb = sb("idb", [128, 128], BF16)
        ntrib = sb("ntrib", [128, 128], BF16)
        nonesb = sb("nonesb", [128, 128], BF16)
        onesb = sb("onesb", [128, 128], BF16)
        mbase = sb("mbase", [128, 2048], BF16)
        cosT = sb("cosT", [128, NT, 8], F32)
        sinT = sb("sinT", [128, NT, 8], F32)
        gmix = sb("gmix", [128, KC], F32)
        gple = sb("gple", [128, KC], F32)
        subg = sb("subg", [128, 1], F32)
        lam = sb("lam", [128, 4], F32)
        mixedT = sb("mixedT", [128, NH, S], BF16)

        def mask_strict(k):
            o = 384 - 128 * k
            return mbase[:, o:o + 512]

        def mask_incl(k):
            o = 1024 + 384 - 128 * k
            return mbase[:, o:o + 512]

        with ExitStack() as s12:
            hnT = sb("hnT", [128, KC, S], BF16, s12)

            with ExitStack() as s1:
                P = Prog(nc, "b1")
                sems = SemPool(nc, top, "b1")
                c32 = sb("c32", [128, 2048], F32, s1)
                c32b = sb("c32b", [128, 128], F32, s1)
                c32c = sb("c32c", [128, 128], F32, s1)
                lamv = sb("lamv", [128, 4, 64], F32, s1)
                lamp = sb("lamp", [128, 2, 64], F32, s1)
                lams = sb("lams", [128, 4], F32, s1)
                xt = [sb(f"xt{i}", [128, D], F32, s1) for i in range(2)]
                xs = [sb(f"xs{i}", [128, D], BF16, s1) for i in range(2)]
                junk = sb("junk", [128, D], BF16, s1)
                stats = sb("stats", [128, NT, 4], F32, s1)

                P.dma("sp", lambda e: e.dma_start(out=c32[:], in_=mbase_d[:, :]), "c0_1", w=["c32"])
                P.dma("sp", lambda e: e.dma_start(out=c32b[:], in_=ident_d[:, :]), "c0_2", w=["c32b"])
                P.dma("sp", lambda e: e.dma_start(out=c32c[:], in_=ntri_d[:, :]), "c0_3", w=["c32c"])
                P.dma("sp", lambda e: e.dma_start(out=cosT[:], in_=cos_d[:, :, :]), "c0_4", w=["cos"])
                P.dma("sp", lambda e: e.dma_start(out=sinT[:], in_=sin_d[:, :, :]), "c0_5", w=["sin"])
                P.dma("sp", lambda e: e.dma_start(out=gmix[:], in_=gmix_d[:, :]), "c0_6", w=["gmix"])
                P.dma("sp", lambda e: e.dma_start(out=gple[:], in_=gple_d[:, :]), "c0_7", w=["gple"])
                P.dma("sp", lambda e: e.dma_start(out=subg[:], in_=subg_d[:, :]), "c0_8", w=["subg"])
                P.dma("sp", lambda e: e.dma_start(out=lamv[:], in_=lamv_d[:, :, :]), "c0_9", w=["lamv"])
                P.pool(lambda e: e.tensor_copy(out=mbase[:], in_=c32[:]), r=["c32"], w=["mbase"])
                P.pool(lambda e: e.tensor_copy(out=idb[:], in_=c32b[:]), r=["c32b"], w=["idb"])
                P.pool(lambda e: e.tensor_copy(out=ntrib[:], in_=c32c[:]), r=["c32c"], w=["ntrib"])
                P.pool(lambda e: e.memset(nonesb[:], -1.0), w=["nonesb"])
                P.pool(lambda e: e.memset(onesb[:], 1.0), w=["onesb"])
                P.dve(lambda e: e.tensor_tensor(out=lamp[:, 0, :], in0=lamv[:, 0, :], in1=lamv[:, 1, :], op=ALU.mult),
                      r=["lamv"], w=["lamp0"])
                P.dve(lambda e: e.tensor_tensor(out=lamp[:, 1, :], in0=lamv[:, 2, :], in1=lamv[:, 3, :], op=ALU.mult),
                      r=["lamv"], w=["lamp1"])
                P.dve(lambda e: e.tensor_reduce(out=lams[:, 0:2], in_=lamp[:], axis=AX.X, op=ALU.add),
                      r=["lamp0", "lamp1"], w=["lams01"])
                P.act(lambda e: e.activation(out=lams[:, 2:4], in_=lams[:, 0:2], func=AF.Exp),
                      r=["lams01"], w=["lams23"])
                P.dve(lambda e: e.tensor_tensor(out=lam[:, 2:3], in0=lams[:, 2:3], in1=lams[:, 3:4], op=ALU.subtract),
                      r=["lams23"], w=["lam2"])
                P.dve(lambda e: e.tensor_scalar(out=lam[:, 0:1], in0=lam[:, 2:3], scalar1=LAMBDA_INIT, scalar2=None,
                                                op0=ALU.add), r=["lam2"], w=["lam0"])
                P.dve(lambda e: e.tensor_scalar(out=lam[:, 1:2], in0=lam[:, 0:1], scalar1=-1.0, scalar2=None,
                                                op0=ALU.mult), r=["lam0"], w=["lam"])

                for i in range(NT):
                    b = i % 2
                    P.dma("sp", lambda e, i=i, b=b: e.dma_start(out=xt[b][:], in_=x_d[i * 128:(i + 1) * 128, :]),
                          f"x{b}", w=[f"xt{b}"])
                    P.act(lambda e, i=i, b=b: e.activation(out=junk[:], in_=xt[b][:], func=AF.Square,
                                                           accum_out=stats[:, i, 0:1]),
                          r=[f"xt{b}"], w=["junk", f"ss{i}"])
                    P.act(lambda e, i=i: e.activation(out=stats[:, i, 1:2], in_=stats[:, i, 0:1], func=AF.Ln,
                                                      scale=1.0 / D, bias=NORM_EPS), r=[f"ss{i}"], w=[f"ln{i}"])
                    P.act(lambda e, i=i: e.activation(out=stats[:, i, 2:3], in_=stats[:, i, 1:2], func=AF.Exp,
                                                      scale=-0.5), r=[f"ln{i}"], w=[f"rs{i}"])
                    P.dve(lambda e, i=i, b=b: e.tensor_scalar(out=xs[b][:], in0=xt[b][:], scalar1=stats[:, i, 2:3],
                                                              scalar2=None, op0=ALU.mult),
                          r=[f"xt{b}", f"rs{i}"], w=[f"xs{b}"])
                    for q4 in range(4):
                        bank = (4 * i + q4) % 8
                        pT = pb[bank][:].rearrange("p (a c) -> p a c", a=4)

                        def tr(e, b=b, q4=q4, pT=pT):
                            for j in range(4):
                                kc = q4 * 4 + j
                                ins = e.matmul(pT[:, j, :], lhsT=xs[b][:, kc * 128:(kc + 1) * 128], rhs=idb[:],
                                               start=True, stop=True)
                            return ins
                        P.pe(tr, r=[f"xs{b}", "idb"], w=[BK(bank)])
                        gb = bass.AP(gmix.tensor if hasattr(gmix, "tensor") else gmix, q4 * 4,
                                     [[KC, 128], [1, 4], [0, 128]])
                        P.dve(lambda e, i=i, q4=q4, pT=pT, gb=gb: e.tensor_tensor(
                            out=hnT[:, q4 * 4:(q4 + 1) * 4, i * 128:(i + 1) * 128], in0=pT, in1=gb, op=ALU.mult),
                            r=[BK(bank), "gmix"], w=[("hnT", i, q4)])
                if dbg_hn:
                    for kc in range(KC):
                        P.dve(lambda e, kc=kc: e.tensor_copy(out=xt[0][:, 0:S], in_=hnT[:, kc, :]),
                              r=[("hnT", i, q4) for i in range(NT) for q4 in range(4)] + ["xt0"], w=["xt0"])
                        P.dma("sp", lambda e, kc=kc: e.dma_start(out=dbg_d[:, kc, :], in_=xt[0][:, 0:S]), "dbg", r=["xt0"])
                P.emit(sems)
            nc.all_engine_barrier()
            if nblocks < 2:
                return nc

            with ExitStack() as s2:
                P = Prog(nc, "b2")
                sems = SemPool(nc, top, "b2")
                wst = [sb(f"wst{i}", [128, KC, 128], F32, s2) for i in range(2)]
                wbf = [sb(f"wbf{i}", [128, KC, 128], BF16, s2) for i in range(2)]
                qT = sb("qT", [128, S], BF16, s2)
                kT = sb("kT", [128, S], BF16, s2)
                kT1 = sb("kT1", [128, S], BF16, s2)
                sgT = sb("sgT", [128, S], BF16, s2)
                vtok = sb("vtok", [128, NT, 128], BF16, s2)
                tokbf = sb("tokbf", [128, NT, 128], BF16, s2)
                xr = sb("xr", [128, NT, 2, 16], F32, s2)
                rt = [sb(f"rt{i}", [128, NT, 2, 8], F32, s2) for i in range(4)]
                e32 = [sb(f"e32_{i}", [128, 512], F32, s2) for i in range(2)]
                spb = [sb(f"spb{i}", [128, 512], BF16, s2) for i in range(2)]
                wTb = [sb(f"wTb{i}", [128, 512], BF16, s2) for i in range(2)]
                ls32 = sb("ls32", [128, 512], F32, s2)
                lsb = [sb(f"lsb{i}", [128, 512], BF16, s2) for i in range(2)]
                Eb = [spb, wTb]
                EbK = [["spb0", "spb1"], ["wTb0", "wTb1"]]
                ep = [e32[0], e32[1]] + [sb(f"ep{i}", [128, 512], F32, s2) for i in range(2, 4)]
                epK = ["e32_0", "e32_1", "ep2", "ep3"]

                for i_ in range(2):
                    P.pool(lambda e, i_=i_: e.memset(e32[i_][:], 0.0), w=[f"e32_{i_}"])
                    P.pool(lambda e, i_=i_: e.memset(spb[i_][:], 0.0), w=[f"spb{i_}"])
                    P.pool(lambda e, i_=i_: e.memset(wTb[i_][:], 0.0), w=[f"wTb{i_}"])

                wstate = {"n": 0}

                def load_wtile(col0):
                    n = wstate["n"]
                    wstate["n"] += 1
                    b = n % 2
                    P.dma("sp", lambda e: e.dma_start(out=wst[b][:], in_=win_v[:, :, col0:col0 + 128]),
                          f"w{b}", w=[f"wst{b}"])
                    castf = P.dve if (n % 2 == 0) else P.pool
                    castf(lambda e: e.tensor_copy(out=wbf[b][:], in_=wst[b][:]), r=[f"wst{b}"], w=[f"wbf{b}"])
                    return b

                def precast(idx):
                    mi, q = idx // 16, idx % 16
                    wv = wout_v if mi == 0 else wgate_v
                    n = wstate["n"]
                    wstate["n"] += 1
                    b = n % 2
                    P.dma("sp", lambda e: e.dma_start(out=wst[b][:], in_=wv[:, :, q * 128:(q + 1) * 128]),
                          f"w{b}", w=[f"wst{b}"])
                    P.pool(lambda e: e.tensor_copy(out=wbf[b][:], in_=wst[b][:]), r=[f"wst{b}"], w=[f"wbf{b}"])
                    dst = wsc_d[mi, q // 2].rearrange("p (kc c) -> p kc c", c=256)[:, :, (q % 2) * 128:(q % 2 + 1) * 128]
                    P.dma("sp", lambda e: e.dma_start(out=dst, in_=wbf[b][:]), f"wo{b}", r=[f"wbf{b}"])

                bankrr = {"n": 0}

                def next_bank():
                    bankrr["n"] += 1
                    return bankrr["n"] % 8

                def proj_feature(col0, evac):
                    b = load_wtile(col0)
                    for tt in range(NTT):
                        bank = next_bank()

                        def mm(e, b=b, tt=tt, bank=bank):
                            for kc in range(KC):
                                ins = e.matmul(pb[bank][:], lhsT=wbf[b][:, kc, :],
                                               rhs=hnT[:, kc, tt * 512:(tt + 1) * 512],
                                               start=(kc == 0), stop=(kc == KC - 1))
                            return ins
                        P.pe(mm, r=[f"wbf{b}"] + [("hnT", i, hf) for i in range(tt * 4, tt * 4 + 4) for hf in range(2)],
                             w=[BK(bank)])
                        evac(tt, bank)

                def proj_token(col0, evac):
                    b = load_wtile(col0)
                    for tt in range(NTT):
                        bank = next_bank()
                        pv = pb[bank][:].rearrange("p (a c) -> p a c", a=4)

                        def mm(e, b=b, tt=tt, pv=pv):
                            for j in range(4):
                                i = tt * 4 + j
                                for kc in range(KC):
                                    ins = e.matmul(pv[:, j, :], lhsT=hnT[:, kc, i * 128:(i + 1) * 128],
                                                   rhs=wbf[b][:, kc, :], start=(kc == 0), stop=(kc == KC - 1))
                            return ins
                        P.pe(mm, r=[f"wbf{b}"] + [("hnT", i, hf) for i in range(tt * 4, tt * 4 + 4) for hf in range(2)],
                             w=[BK(bank)])
                        evac(tt, bank, pv)

                def ev_q_sb(tt, bank):
                    P.act(lambda e: e.activation(out=qT[:, tt * 512:(tt + 1) * 512], in_=pb[bank][:], func=AF.Copy,
                                                 scale=SB_SCALE), r=[BK(bank)], w=[("qT", tt)])

                def ev_k(tt, bank):
                    P.dve(lambda e: e.tensor_copy(out=kT[:, tt * 512:(tt + 1) * 512], in_=pb[bank][:]),
                          r=[BK(bank)], w=[("kT", tt)])

                def ev_g(tt, bank):
                    P.act(lambda e: e.activation(out=sgT[:, tt * 512:(tt + 1) * 512], in_=pb[bank][:], func=AF.Silu),
                          r=[BK(bank)], w=[("sgT", tt)])

                def ev_v(tt, bank, pv):
                    P.dve(lambda e: e.tensor_copy(out=vtok[:, tt * 4:(tt + 1) * 4, :], in_=pv),
                          r=[BK(bank)], w=[("vtok", tt)])

                def ev_tok(tt, bank, pv):
                    P.dve(lambda e: e.tensor_copy(out=tokbf[:, tt * 4:(tt + 1) * 4, :], in_=pv),
                          r=[BK(bank)], w=[("tokbf", tt)])
                    src = pv.rearrange("p a (m d) -> p a m d", m=2)[:, :, :, 0:16]
                    P.dve(lambda e: e.tensor_copy(out=xr[:, tt * 4:(tt + 1) * 4, :, :], in_=src),
                          r=[BK(bank)], w=[("xr", tt)])

                def rope_and_transpose(dstT, dstname, split=False):
                    x1 = xr[:, :, :, 0:8]
                    x2 = xr[:, :, :, 8:16]
                    cb = bass.AP(cosT.tensor if hasattr(cosT, "tensor") else cosT, 0,
                                 [[NT * 8, 128], [8, NT], [0, 2], [1, 8]])
                    sbp = bass.AP(sinT.tensor if hasattr(sinT, "tensor") else sinT, 0,
                                  [[NT * 8, 128], [8, NT], [0, 2], [1, 8]])
                    xrk = [("xr", tt) for tt in range(NTT)]
                    tk = [("tokbf", tt) for tt in range(NTT)]
                    P.dve(lambda e: e.tensor_tensor(out=rt[0][:], in0=x1, in1=cb, op=ALU.mult), r=xrk + ["cos"], w=["rt0"])
                    P.dve(lambda e: e.tensor_tensor(out=rt[1][:], in0=x2, in1=sbp, op=ALU.mult), r=xrk + ["sin"], w=["rt1"])
                    P.dve(lambda e: e.tensor_tensor(out=rt[2][:], in0=x2, in1=cb, op=ALU.mult), r=xrk + ["cos"], w=["rt2"])
                    P.dve(lambda e: e.tensor_tensor(out=rt[3][:], in0=x1, in1=sbp, op=ALU.mult), r=xrk + ["sin"], w=["rt3"])
                    tv = tokbf[:].rearrange("p a (m d) -> p a m d", m=2)
                    P.dve(lambda e: e.tensor_tensor(out=tv[:, :, :, 0:8], in0=rt[0][:], in1=rt[1][:], op=ALU.subtract),
                          r=["rt0", "rt1"] + tk, w=tk)
                    P.dve(lambda e: e.tensor_tensor(out=tv[:, :, :, 8:16], in0=rt[2][:], in1=rt[3][:], op=ALU.add),
                          r=["rt2", "rt3"] + tk, w=tk)
                    for tt in range(NTT):
                        bank = next_bank()
                        pT = pb[bank][:].rearrange("p (a c) -> p a c", a=4)

                        def tr(e, tt=tt, pT=pT):
                            for j in range(4):
                                ins = e.matmul(pT[:, j, :], lhsT=tokbf[:, tt * 4 + j, :], rhs=idb[:], start=True, stop=True)
                            return ins
                        P.pe(tr, r=tk + ["idb"], w=[BK(bank)])
                        if not split:
                            P.act(lambda e, tt=tt, bank=bank: e.activation(
                                out=dstT[:, tt * 512:(tt + 1) * 512], in_=pb[bank][:], func=AF.Copy),
                                r=[BK(bank)], w=[(dstname, tt)])
                        else:
                            P.act(lambda e, tt=tt, bank=bank: e.activation(
                                out=kT[0:64, tt * 512:(tt + 1) * 512], in_=pb[bank][0:64, :], func=AF.Copy),
                                r=[BK(bank)], w=[("kT", tt)])
                            P.act(lambda e, tt=tt, bank=bank: e.activation(
                                out=kT1[64:128, tt * 512:(tt + 1) * 512], in_=pb[bank][64:128, :], func=AF.Copy),
                                r=[BK(bank)], w=[("kT1", tt)])

                def sb_attention(h):
                    pairs = []
                    for T in range(NTT):
                        for c in range(4 * T + 3, -1, -1):
                            pairs.append((T, c))
                    zb = [next_bank(), next_bank()]
                    bb = [next_bank(), next_bank()]
                    ob = [next_bank(), next_bank()]
                    npair = len(pairs)

                    def emit_z(n):
                        T, c = pairs[n]
                        bank = zb[n % 2]
                        P.pe(lambda e: e.matmul(pb[bank][:], lhsT=kT[:, c * 128:(c + 1) * 128],
                                                rhs=qT[:, T * 512:(T + 1) * 512], start=True, stop=True),
                             r=[("kT", c // 4), ("qT", T)], w=[BK(bank)])

                    def emit_w(n):
                        T, c = pairs[n]
                        k = c - 4 * T
                        first = (c == 4 * T + 3)
                        last = (c == 0)
                        c0w = 128 * k if k > 0 else 0
                        P.act(lambda e: e.activation(out=wTb[n % 2][:, c0w:512], in_=pb[bb[n % 2]][:, c0w:512], func=AF.Exp),
                              r=[BK(bb[n % 2])], w=[f"wTb{n % 2}"])
                        if k >= 0:
                            P.dve(lambda e: e.tensor_tensor(out=wTb[n % 2][:], in0=wTb[n % 2][:], in1=mask_strict(k),
                                                             op=ALU.mult), r=[f"wTb{n % 2}", "mbase"], w=[f"wTb{n % 2}"])
                        P.pe(lambda e: e.matmul(pb[ob[T % 2]][:], lhsT=vtok[:, c, :], rhs=wTb[n % 2][:],
                                                start=first, stop=last),
                             r=[("vtok", c // 4), f"wTb{n % 2}"], w=[BK(ob[T % 2])], defer=(not last))
                        if last:
                            P.dve(lambda e: e.tensor_tensor(out=mixedT[:, h, T * 512:(T + 1) * 512], in0=pb[ob[T % 2]][:],
                                                            in1=sgT[:, T * 512:(T + 1) * 512], op=ALU.mult),
                                  r=[BK(ob[T % 2]), ("sgT", T)], w=[("mixedT", h, T)])

                    def col0(n):
                        T_, c_ = pairs[n]
                        k_ = c_ - 4 * T_
                        return 128 * k_ if k_ > 0 else 0

                    def emit_e(n):
                        zbank = zb[n % 2]
                        i_ = n % 2
                        c0 = col0(n)
                        P.act(lambda e: e.activation(out=e32[i_][:, c0:512], in_=pb[zbank][:, c0:512], func=AF.Exp),
                              r=[BK(zbank)], w=[f"e32_{i_}"])

                    emit_z(0)
                    if npair > 1:
                        emit_z(1)
                    emit_e(0)
                    for n in range(npair):
                        T, c = pairs[n]
                        k = c - 4 * T
                        first = (c == 4 * T + 3)
                        if n + 2 < npair:
                            emit_z(n + 2)
                        if n + 1 < npair:
                            emit_e(n + 1)
                        sbuf_i = n % 2
                        c0n = col0(n)
                        P.act(lambda e, sbuf_i=sbuf_i, c0n=c0n: e.activation(out=spb[sbuf_i][:, c0n:512], in_=e32[sbuf_i][:, c0n:512],
                                                                            func=AF.Ln, bias=1.0),
                              r=[f"e32_{sbuf_i}"], w=[f"spb{sbuf_i}"])
                        if k >= 0:
                            P.dve(lambda e, sbuf_i=sbuf_i, k=k: e.tensor_tensor(out=spb[sbuf_i][:], in0=spb[sbuf_i][:],
                                                                                in1=mask_strict(k), op=ALU.mult),
                                  r=[f"spb{sbuf_i}", "mbase"], w=[f"spb{sbuf_i}"])
                        m = (4 * T + 3) - c
                        bbank = bb[n % 2]
                        if m == 0:
                            lsrc, lkey = None, None
                        elif m == 1:
                            lsrc, lkey = spb[(n - 1) % 2], f"spb{(n - 1) % 2}"
                        else:
                            lsrc, lkey = lsb[m % 2], f"lsb{m % 2}"

                        def bmm(e, T=T, c=c, sbuf_i=sbuf_i, bbank=bbank, lsrc=lsrc):
                            e.matmul(pb[bbank][:], lhsT=kT[:, c * 128:(c + 1) * 128],
                                     rhs=qT[:, T * 512:(T + 1) * 512], start=True, stop=False)
                            ins = e.matmul(pb[bbank][:], lhsT=ntrib[:], rhs=spb[sbuf_i][:], start=False,
                                           stop=(lsrc is None))
                            if lsrc is not None:
                                ins = e.matmul(pb[bbank][:], lhsT=nonesb[:], rhs=lsrc[:], start=False, stop=True)
                            return ins
                        if sbstage >= 6:
                            P.pe(bmm, r=[("kT", c // 4), ("qT", T), "ntrib", "nonesb", f"spb{sbuf_i}"] + ([lkey] if lkey else []),
                                 w=[BK(bbank)])
                        if c > 0 and m >= 1 and sbstage >= 7:
                            if m == 1:
                                P.dve(lambda e, n=n: e.tensor_tensor(out=ls32[:], in0=spb[(n - 1) % 2][:], in1=spb[n % 2][:],
                                                                     op=ALU.add),
                                      r=[f"spb{(n - 1) % 2}", f"spb{n % 2}"], w=["ls32"])
                            else:
                                P.dve(lambda e, n=n: e.tensor_tensor(out=ls32[:], in0=ls32[:], in1=spb[n % 2][:], op=ALU.add),
                                      r=["ls32", f"spb{n % 2}"], w=["ls32"])
                            P.dve(lambda e, m=m: e.tensor_copy(out=lsb[(m + 1) % 2][:], in_=ls32[:]),
                                  r=["ls32"], w=[f"lsb{(m + 1) % 2}"])
                        if n >= 1 and sbstage >= 8:
                            emit_w(n - 1)
                    if sbstage >= 8:
                        emit_w(npair - 1)

                def df_attention(h):
                    pairs = []
                    for T in range(NTT):
                        for c in range(0, 4 * T + 4):
                            pairs.append((T, c))
                    npair = len(pairs)
                    sbk = [[next_bank(), next_bank()] for _ in range(2)]
                    obk = [next_bank(), next_bank()]
                    zbk = [next_bank(), next_bank()]

                    def emit_s(n):
                        T, c = pairs[n]
                        for m in range(2):
                            bank = sbk[m][n % 2]
                            P.pe(lambda e, m=m, bank=bank: e.matmul(
                                pb[bank][:], lhsT=(kT if m == 0 else kT1)[:, c * 128:(c + 1) * 128],
                                rhs=qT[:, T * 512:(T + 1) * 512], start=True, stop=True),
                                r=[("kT", c // 4), ("kT1", c // 4), ("qT", T)], w=[BK(bank)])

                    emit_s(0)
                    pending = None
                    for n in range(npair):
                        T, c = pairs[n]
                        k = c - 4 * T
                        first = (c == 0)
                        last = (c == 4 * T + 3)
                        if n + 1 < npair:
                            emit_s(n + 1)
                        for m in range(2):
                            bank = sbk[m][n % 2]
                            c0d = 128 * k if k > 0 else 0
                            P.act(lambda e, m=m, bank=bank, n=n, c0d=c0d: e.activation(out=Eb[m][n % 2][:, c0d:512],
                                                                                       in_=pb[bank][:, c0d:512],
                                                                                       func=AF.Exp, scale=DF_SCALE),
                                  r=[BK(bank)], w=[EbK[m][n % 2]])
                            if k >= 0:
                                mf = P.dve
                                mf(lambda e, m=m, n=n, k=k: e.tensor_tensor(out=Eb[m][n % 2][:], in0=Eb[m][n % 2][:],
                                                                            in1=mask_incl(k), op=ALU.mult),
                                   r=[EbK[m][n % 2], "mbase"], w=[EbK[m][n % 2]])
                        for m in (range(2) if dfstage >= 6 else []):
                            P.pe(lambda e, m=m, n=n, c=c, first=first, last=last: e.matmul(
                                pb[obk[m]][:], lhsT=vtok[:, c, :], rhs=Eb[m][n % 2][:], start=first, stop=last),
                                r=[("vtok", c // 4), EbK[m][n % 2]], w=[BK(obk[m])], defer=(not last))
                            P.pe(lambda e, m=m, n=n, first=first, last=last: e.matmul(
                                pb[zbk[m]][:], lhsT=onesb[:], rhs=Eb[m][n % 2][:], start=first, stop=last),
                                r=["onesb", EbK[m][n % 2]], w=[BK(zbk[m])], defer=(not last))
                        if pending is not None:
                            pending(sbk[0][n % 2])
                            pending = None
                        if last and dfstage >= 7:
                            for m in range(2):
                                P.act(lambda e, m=m: e.activation(out=ep[m][:], in_=pb[obk[m]][:], func=AF.Copy),
                                      r=[BK(obk[m])], w=[epK[m]])
                                P.dve(lambda e, m=m: e.reciprocal(out=ep[2 + m][:], in_=pb[zbk[m]][:]),
                                      r=[BK(zbk[m])], w=[epK[2 + m]])

                            def part2(msbank, T=T):
                                ts = slice(T * 512, (T + 1) * 512)
                                for m in range(2):
                                    P.dve(lambda e, m=m: e.tensor_tensor(out=ep[m][:], in0=ep[m][:], in1=ep[2 + m][:], op=ALU.mult),
                                          r=[epK[m], epK[2 + m]], w=[epK[m]])
                                P.dve(lambda e: e.scalar_tensor_tensor(out=ep[2][:], in0=ep[1][:], scalar=lam[:, 1:2], in1=ep[0][:],
                                                                       op0=ALU.mult, op1=ALU.add),
                                      r=[epK[0], epK[1], "lam"], w=["ep2"])
                                d2b = lsb[0]
                                P.dve(lambda e: e.tensor_tensor(out=d2b[:], in0=ep[2][:], in1=ep[2][:], op=ALU.mult),
                                      r=["ep2"], w=["lsb0"])
                                P.pe(lambda e: e.matmul(pb[msbank][:], lhsT=onesb[:], rhs=d2b[:], start=True, stop=True),
                                     r=["onesb", "lsb0"], w=[BK(msbank)])
                                P.act(lambda e: e.activation(out=ep[0][:], in_=pb[msbank][:], func=AF.Ln, scale=1.0 / 128.0,
                                                             bias=SUBLN_EPS),
                                      r=[BK(msbank)], w=[epK[0]])
                                P.act(lambda e: e.activation(out=ep[1][:], in_=ep[0][:], func=AF.Exp, scale=-0.5,
                                                             bias=math.log(1.0 - LAMBDA_INIT)),
                                      r=[epK[0]], w=[epK[1]])
                                P.dve(lambda e: e.tensor_tensor(out=ep[2][:], in0=ep[2][:], in1=ep[1][:], op=ALU.mult),
                                      r=["ep2", epK[1]], w=["ep2"])
                                P.dve(lambda e, ts=ts: e.scalar_tensor_tensor(out=mixedT[:, n_sb_off + h, ts], in0=ep[2][:],
                                                                              scalar=subg[:, 0:1], in1=sgT[:, ts],
                                                                              op0=ALU.mult, op1=ALU.mult),
                                      r=["ep2", "subg", ("sgT", T)], w=[("mixedT", n_sb_off + h, T)])
                            if n == npair - 1:
                                part2(sbk[0][n % 2])
                            else:
                                pending = part2

                n_sb_off = 8
                PH = {"P": P}

                def new_block(tag):
                    return

                for h in range(n_sb):
                    proj_feature(0 * 1024 + h * 128, ev_q_sb)
                    if stage >= 2:
                        proj_feature(1 * 1024 + h * 128, ev_k)
                        proj_token(2 * 1024 + h * 128, ev_v)
                    if stage >= 3:
                        proj_feature(3 * 1024 + h * 128, ev_g)
                    if nblocks >= 3:
                        precast(2 * h)
                        precast(2 * h + 1)
                    if stage >= 4:
                        sb_attention(h)
                    new_block(f"sb{h}")
                if n_df > 0:
                    P.pool(lambda e: e.memset(kT[64:128, :], 0.0), w=[("kT", tt) for tt in range(NTT)])
                    P.pool(lambda e: e.memset(kT1[0:64, :], 0.0), w=[("kT1", tt) for tt in range(NTT)])
                for h in range(n_df):
                    proj_token(4 * 1024 + h * 128, ev_tok)
                    proj_token(6 * 1024 + h * 128, ev_v)
                    rope_and_transpose(qT, "qT")
                    proj_token(5 * 1024 + h * 128, ev_tok)
                    proj_feature(7 * 1024 + h * 128, ev_g)
                    rope_and_transpose(kT, "kT", split=True)
                    new_block(f"dfp{h}")
                    if nblocks >= 3:
                        precast(16 + 2 * h)
                        precast(16 + 2 * h + 1)
                    if stage >= 4:
                        df_attention(h)
                    new_block(f"dfa{h}")
                if debug_mixed:
                    dbg32 = sb("dbg32", [128, S], F32, s2)
                    for hc in (list(range(n_sb)) + list(range(8, 8 + n_df))) if stage >= 4 else []:
                        P.dve(lambda e, hc=hc: e.tensor_copy(out=dbg32[:], in_=mixedT[:, hc, :]),
                              r=[("mixedT", hc, T) for T in range(NTT)], w=["dbg32"])
                        P.dma("sp", lambda e, hc=hc: e.dma_start(out=dbg_d[:, hc, :], in_=dbg32[:]), "dbg",
                              r=["dbg32"])
                P.emit(sems)
            nc.all_engine_barrier()

        if nblocks < 3:
            return nc
        with ExitStack() as s3:
            P = Prog(nc, "b3")
            sems = SemPool(nc, top, "b3")
            WC = 256
            NWC = D // WC
            wbf = [sb(f"wbf3_{i}", [128, KC, WC], BF16, s3) for i in range(2)]
            wpj32 = sb("wpj32", [128, 2, 512], F32, s3)
            wpj = sb("wpj", [128, 2, D], BF16, s3)
            gfin = sb("gfin", [128, D], F32, s3)
            hh = sb("hh", [128, 4, D], F32, s3)
            xt = [sb(f"xt3_{i}", [128, D], F32, s3) for i in range(2)]
            hs = sb("hs", [128, D], BF16, s3)
            hn2T = sb("hn2T", [128, KC, 512], BF16, s3)
            pt32 = sb("pt32", [128, PLE], F32, s3)
            ptb = sb("ptb", [128, PLE], BF16, s3)
            pT = sb("pT", [128, 2, 512], BF16, s3)
            gt = [sb(f"gt{i}", [128, WC], F32, s3) for i in range(2)]
            junk = sb("junk3", [128, D], BF16, s3)
            stats = sb("stats3", [128, NT, 8], F32, s3)

            bankrr = {"n": 0}

            def next_bank():
                bankrr["n"] += 1
                return bankrr["n"] % 8

            wstate = {"n": 0, "t": 0}

            def load_wtile3(wv, col0):
                t = wstate["t"]
                wstate["t"] += 1
                b = t % 2
                mi = 0 if wv is wout_v else 1
                src = wsc_d[mi, col0 // WC].rearrange("p (kc c) -> p kc c", c=256)
                P.dma("sp", lambda e: e.dma_start(out=wbf[b][:], in_=src), f"w3_{b}",
                      w=[(f"wbf{b}", q) for q in range(WC // 128)])
                return b

            P.dma("sp", lambda e: e.dma_start(out=gfin[:], in_=gfin_d[:, :]), "c3", w=["gfin"])
            for q in range(D // 512):
                P.dma("sp", lambda e, q=q: e.dma_start(out=wpj32[:], in_=wproj_v[:, :, q * 512:(q + 1) * 512]), "wpj",
                      w=["wpj32"])
                P.dve(lambda e, q=q: e.tensor_copy(out=wpj[:, :, q * 512:(q + 1) * 512], in_=wpj32[:]),
                      r=["wpj32"], w=[("wpj", q)])
            xn = {"n": 0}

            for T in range(NTT):
                for wc in range(NWC):
                    b = load_wtile3(wout_v, wc * WC)
                    for j in range(4):
                        i = T * 4 + j
                        bank = next_bank()

                        def mm(e, b=b, i=i, bank=bank):
                            for kc in range(KC):
                                ins = e.matmul(pb[bank][:, 0:WC], lhsT=mixedT[:, kc, i * 128:(i + 1) * 128],
                                               rhs=wbf[b][:, kc, :], start=(kc == 0), stop=(kc == KC - 1))
                            return ins
                        P.pe(mm, r=[(f"wbf{b}", q) for q in range(WC // 128)], w=[BK(bank)])
                        P.act(lambda e, j=j, wc=wc, bank=bank: e.activation(out=hh[:, j, wc * WC:(wc + 1) * WC],
                                                                            in_=pb[bank][:, 0:WC], func=AF.Copy),
                              r=[BK(bank)], w=[("hh", j, wc)])
                for j in range(4):
                    i = T * 4 + j
                    xb = xn["n"] % 2
                    xn["n"] += 1
                    P.dma("sp", lambda e, i=i, xb=xb: e.dma_start(out=xt[xb][:], in_=x_d[i * 128:(i + 1) * 128, :]),
                          f"x3_{xb}", w=[f"xt{xb}"])
                    hk = [("hh", j, wc) for wc in range(NWC)]
                    P.dve(lambda e, j=j, xb=xb: e.tensor_tensor(out=hh[:, j, :], in0=hh[:, j, :], in1=xt[xb][:], op=ALU.add),
                          r=hk + [f"xt{xb}"], w=hk)
                    P.act(lambda e, j=j, i=i: e.activation(out=junk[:], in_=hh[:, j, :], func=AF.Square,
                                                           accum_out=stats[:, i, 0:1]), r=hk, w=["junk", ("ss", i)])
                    P.act(lambda e, i=i: e.activation(out=stats[:, i, 1:2], in_=stats[:, i, 0:1], func=AF.Ln,
                                                      scale=1.0 / D, bias=NORM_EPS), r=[("ss", i)], w=[("ln", i)])
                    P.act(lambda e, i=i: e.activation(out=stats[:, i, 2:3], in_=stats[:, i, 1:2], func=AF.Exp,
                                                      scale=-0.5), r=[("ln", i)], w=[("rs", i)])
                    P.dve(lambda e, j=j, i=i: e.tensor_scalar(out=hs[:], in0=hh[:, j, :], scalar1=stats[:, i, 2:3],
                                                              scalar2=None, op0=ALU.mult), r=hk + [("rs", i)], w=["hs"])
                    for q4 in range(4):
                        bank = next_bank()
                        pTr = pb[bank][:].rearrange("p (a c) -> p a c", a=4)

                        def tr(e, q4=q4, pTr=pTr):
                            for jj in range(4):
                                kc = q4 * 4 + jj
                                ins = e.matmul(pTr[:, jj, :], lhsT=hs[:, kc * 128:(kc + 1) * 128], rhs=idb[:],
                                               start=True, stop=True)
                            return ins
                        P.pe(tr, r=["hs", "idb"], w=[BK(bank)])
                        gb = bass.AP(gple.tensor if hasattr(gple, "tensor") else gple, q4 * 4,
                                     [[KC, 128], [1, 4], [0, 128]])
                        P.dve(lambda e, j=j, q4=q4, pTr=pTr, gb=gb: e.tensor_tensor(
                            out=hn2T[:, q4 * 4:(q4 + 1) * 4, j * 128:(j + 1) * 128], in0=pTr, in1=gb, op=ALU.mult),
                            r=[BK(bank), "gple"], w=[("hn2T", j, q4)])
                    P.dma("sp", lambda e, i=i: e.dma_start(out=pt32[:], in_=p_d[i * 128:(i + 1) * 128, :]), "p3",
                          w=["pt32"])
                    P.pool(lambda e: e.tensor_copy(out=ptb[:], in_=pt32[:]), r=["pt32"], w=["ptb"])
                    bank = next_bank()
                    pPr = pb[bank][:, 0:256].rearrange("p (a c) -> p a c", a=2)

                    def trp(e, pPr=pPr):
                        for jj in range(2):
                            ins = e.matmul(pPr[:, jj, :], lhsT=ptb[:, jj * 128:(jj + 1) * 128], rhs=idb[:], start=True, stop=True)
                        return ins
                    P.pe(trp, r=["ptb", "idb"], w=[BK(bank)])
                    P.act(lambda e, j=j, pPr=pPr: e.activation(out=pT[:, :, j * 128:(j + 1) * 128], in_=pPr, func=AF.Copy),
                          r=[BK(bank)], w=[("pT", j)])
                for wc in range(NWC):
                    b = load_wtile3(wgate_v, wc * WC)
                    for j in range(4):
                        i = T * 4 + j
                        bank = next_bank()
                        bank2 = next_bank()

                        def mm(e, b=b, j=j, bank=bank):
                            for kc in range(KC):
                                ins = e.matmul(pb[bank][:, 0:WC], lhsT=hn2T[:, kc, j * 128:(j + 1) * 128],
                                               rhs=wbf[b][:, kc, :], start=(kc == 0), stop=(kc == KC - 1))
                            return ins
                        P.pe(mm, r=[(f"wbf{b}", q) for q in range(WC // 128)] + [("hn2T", j, q4) for q4 in range(4)],
                             w=[BK(bank)])

                        def mm2(e, j=j, wc=wc, bank2=bank2):
                            for kc in range(2):
                                ins = e.matmul(pb[bank2][:, 0:WC], lhsT=pT[:, kc, j * 128:(j + 1) * 128],
                                               rhs=wpj[:, kc, wc * WC:(wc + 1) * WC], start=(kc == 0), stop=(kc == 1))
                            return ins
                        P.pe(mm2, r=[("pT", j)] + [("wpj", q) for q in range(D // 512)], w=[BK(bank2)])
                        g = (wc * 4 + j) % 2
                        P.act(lambda e, g=g, bank=bank: e.activation(out=gt[g][:], in_=pb[bank][:, 0:WC], func=AF.Sigmoid),
                              r=[BK(bank)], w=[f"gt{g}"])
                        P.dve(lambda e, g=g, bank2=bank2: e.tensor_tensor(out=gt[g][:], in0=gt[g][:], in1=pb[bank2][:, 0:WC],
                                                                          op=ALU.mult), r=[f"gt{g}", BK(bank2)], w=[f"gt{g}"])
                        P.dve(lambda e, g=g, j=j, wc=wc: e.tensor_tensor(out=hh[:, j, wc * WC:(wc + 1) * WC],
                                                                         in0=hh[:, j, wc * WC:(wc + 1) * WC], in1=gt[g][:],
                                                                         op=ALU.add),
                              r=[f"gt{g}", ("hh", j, wc)], w=[("hh", j, wc)])
                for j in range(4):
                    i = T * 4 + j
                    hk = [("hh", j, wc) for wc in range(NWC)]
                    P.act(lambda e, j=j, i=i: e.activation(out=junk[:], in_=hh[:, j, :], func=AF.Square,
                                                           accum_out=stats[:, i, 4:5]), r=hk, w=["junk", ("ss2", i)])
                    P.act(lambda e, i=i: e.activation(out=stats[:, i, 5:6], in_=stats[:, i, 4:5], func=AF.Ln,
                                                      scale=1.0 / D, bias=NORM_EPS), r=[("ss2", i)], w=[("ln2", i)])
                    P.act(lambda e, i=i: e.activation(out=stats[:, i, 6:7], in_=stats[:, i, 5:6], func=AF.Exp,
                                                      scale=-0.5), r=[("ln2", i)], w=[("rs2", i)])
                    xb = xn["n"] % 2
                    xn["n"] += 1
                    P.dve(lambda e, j=j, i=i, xb=xb: e.scalar_tensor_tensor(out=xt[xb][:], in0=hh[:, j, :],
                                                                            scalar=stats[:, i, 6:7], in1=gfin[:],
                                                                            op0=ALU.mult, op1=ALU.mult),
                          r=hk + [("rs2", i), "gfin"], w=[f"xt{xb}"])
                    P.dma("sp", lambda e, i=i, xb=xb: e.dma_start(out=y_d[i * 128:(i + 1) * 128, :], in_=xt[xb][:]),
                          f"x3_{xb}", r=[f"xt{xb}"])
            P.emit(sems)
    return nc


def make_consts(S):
    NT = S // 128
    half = 8
    inv_freq = (500000.0 ** (-np.arange(0, 16, 2, dtype=np.float32) / np.float32(16))).astype(np.float32)
    pos = np.arange(S, dtype=np.float32)
    ang = (pos[:, None] * inv_freq[None, :]).astype(np.float32)
    cos = np.cos(ang).astype(np.float32).reshape(NT, 128, half).transpose(1, 0, 2)
    sin = np.sin(ang).astype(np.float32).reshape(NT, 128, half).transpose(1, 0, 2)
    ident = np.eye(128, dtype=np.float32)
    j = np.arange(128)[:, None]
    s = np.arange(128)[None, :]
    ntri = np.where(j >= s, -1.0, 0.0).astype(np.float32)
    u = np.arange(1024)[None, :]
    mbase = np.concatenate([np.where(j < u - 384, 1.0, 0.0), np.where(j <= u - 384, 1.0, 0.0)], axis=1).astype(np.float32)
    return {
        "cos": np.ascontiguousarray(cos), "sin": np.ascontiguousarray(sin),
        "ident": ident, "ntri": ntri, "mbase": mbase,
    }


_NC_CACHE = {}


def kernel(x, p, norm_mix_g, w_in, lambda_q1, lambda_k1, lambda_q2, lambda_k2, subln_g,
           w_out, norm_ple_g, w_ple_gate, w_ple_proj, norm_final_g):
    x = np.asarray(x, dtype=np.float32)
    p = np.asarray(p, dtype=np.float32)
    B, S, _ = x.shape
    consts = make_consts(S)
    f = lambda a: np.ascontiguousarray(np.asarray(a, dtype=np.float32))
    shared = {
        "w_in": f(w_in[0]), "w_out": f(w_out[0]), "w_gate": f(w_ple_gate[0]), "w_proj": f(w_ple_proj[0]),
        "gmix": f(np.asarray(norm_mix_g[0]).reshape(KC, 128).T),
        "gple": f(np.asarray(norm_ple_g[0]).reshape(KC, 128).T),
        "gfin": f(np.broadcast_to(np.asarray(norm_final_g)[None, :], (128, D))),
        "subg": f(np.asarray(subln_g[0]).reshape(128, 1)),
        "lamv": f(np.broadcast_to(np.stack([np.asarray(lambda_q1[0]), np.asarray(lambda_k1[0]),
                                            np.asarray(lambda_q2[0]), np.asarray(lambda_k2[0])])[None], (128, 4, 64))),
    }
    shared.update(consts)
    if S not in _NC_CACHE:
        _NC_CACHE[S] = build_nc(S)
    nc = _NC_CACHE[S]
    in_maps = []
    for b in range(B):
        m = dict(shared)
        m["x"] = f(x[b])
        m["p"] = f(p[0, b])
        in_maps.append(m)
    res = run_bass_kernel_spmd(nc, in_maps, core_ids=list(range(B)))
    out = np.stack([np.asarray(r["y"]).reshape(S, D) for r in res.results], axis=0)
    return out.astype(np.float32)
```
